# Optimizing a Trainium2 kernel written in Bass

```python
import jax, jax.numpy as jnp
from jax import lax
import numpy as np

D_MODEL = 2048
BATCH = 4
SEQ = 4096
DEPTH = 1

CHUNK = 64
N_PREV_CHUNKS = 8
BAND_CHUNKS = N_PREV_CHUNKS + 1
BAND_LEN = BAND_CHUNKS * CHUNK
ATT_HEADS = 8
ATT_HEAD_DIM = 128
ATT_WIDTH = ATT_HEADS * ATT_HEAD_DIM
MAX_REL_DIST = 128
N_REL = 2 * MAX_REL_DIST + 1
RET_HEADS = 8
RET_KEY_DIM = 128
RET_VAL_DIM = 128
RET_QK_WIDTH = RET_HEADS * RET_KEY_DIM
RET_V_WIDTH = RET_HEADS * RET_VAL_DIM
ROPE_BASE = 10000.0
D_FF = 5632
EPS = 1e-6
IN_SPLIT_WIDTHS = (ATT_WIDTH, ATT_WIDTH, ATT_WIDTH, RET_QK_WIDTH, RET_QK_WIDTH,
                   RET_V_WIDTH, RET_V_WIDTH, D_MODEL, D_MODEL)
IN_WIDTH = sum(IN_SPLIT_WIDTHS)
IN_SPLIT_POINTS = tuple(int(v) for v in np.cumsum(IN_SPLIT_WIDTHS)[:-1])

kernel_name = "macaron_gated_chunkattn_retention_block"


def rmsnorm(x, g):
    xf = x.astype(jnp.float32)
    y = xf * lax.rsqrt(jnp.mean(xf * xf, axis=-1, keepdims=True) + EPS)
    return (y * g.astype(jnp.float32)).astype(x.dtype)


def swiglu_ffn(x, w_gate, w_up, w_down):
    return (jax.nn.silu(x @ w_gate) * (x @ w_up)) @ w_down


def rotary(x):
    S, d = x.shape[1], x.shape[-1]
    inv = 1.0 / (ROPE_BASE ** (jnp.arange(0, d, 2, dtype=jnp.float32) / d))
    ang = jnp.arange(S, dtype=jnp.float32)[:, None] * inv[None, :]
    cos = jnp.cos(ang)[None, :, None, :].astype(x.dtype)
    sin = jnp.sin(ang)[None, :, None, :].astype(x.dtype)
    x1, x2 = x[..., : d // 2], x[..., d // 2:]
    return jnp.concatenate([x1 * cos - x2 * sin, x1 * sin + x2 * cos], axis=-1)


def chunked_band_attention(q, k, v, rel_bias_table):
    B, S, H, hd = q.shape
    nc = S // CHUNK
    qc = q.reshape(B, nc, CHUNK, H, hd)
    kc = k.reshape(B, nc, CHUNK, H, hd)
    vc = v.reshape(B, nc, CHUNK, H, hd)
    pad = ((0, 0), (N_PREV_CHUNKS, 0), (0, 0), (0, 0), (0, 0))
    kp = jnp.pad(kc, pad)
    vp = jnp.pad(vc, pad)
    k_band = jnp.concatenate([kp[:, j:j + nc] for j in range(BAND_CHUNKS)], axis=2)
    v_band = jnp.concatenate([vp[:, j:j + nc] for j in range(BAND_CHUNKS)], axis=2)
    scores = jnp.einsum('bcqhd,bckhd->bhcqk', qc, k_band,
                        preferred_element_type=jnp.float32) * (hd ** -0.5)
    q_pos = jnp.arange(CHUNK) + N_PREV_CHUNKS * CHUNK
    k_pos = jnp.arange(BAND_LEN)
    rel_idx = jnp.clip(q_pos[:, None] - k_pos[None, :], -MAX_REL_DIST, MAX_REL_DIST) + MAX_REL_DIST
    bias = rel_bias_table.astype(jnp.float32)[:, rel_idx]
    band_chunk = k_pos // CHUNK
    valid = (jnp.arange(nc)[:, None] - N_PREV_CHUNKS + band_chunk[None, :]) >= 0
    scores = scores + bias[None, :, None, :, :]
    scores = jnp.where(valid[None, None, :, None, :], scores, jnp.float32(-1e30))
    p = jax.nn.softmax(scores, axis=-1).astype(v.dtype)
    out = jnp.einsum('bhcqk,bckhd->bcqhd', p, v_band)
    return out.reshape(B, S, H * hd)


def chunkwise_retention(q, k, v):
    B, S, H, dk = q.shape
    dv = v.shape[-1]
    nc = S // CHUNK
    f32 = jnp.float32
    gamma = 1.0 - 2.0 ** (-5.0 - jnp.arange(H, dtype=f32))
    log_g = jnp.log(gamma)
    idx = jnp.arange(CHUNK, dtype=f32)
    intra = jnp.exp(log_g[:, None, None] * jnp.abs(idx[:, None] - idx[None, :]))
    q_decay = jnp.exp(log_g[:, None] * (idx[None, :] + 1.0))
    k_decay = jnp.exp(log_g[:, None] * (CHUNK - 1.0 - idx[None, :]))
    chunk_decay = jnp.exp(log_g * CHUNK)

    def to_chunks(t):
        return t.astype(f32).reshape(B, nc, CHUNK, H, t.shape[-1]).transpose(1, 0, 3, 2, 4)

    qc = to_chunks(q) * (dk ** -0.5)
    kc = to_chunks(k)
    vc = to_chunks(v)

    def step(state, inp):
        qi, ki, vi = inp
        s = jnp.einsum('bhqd,bhkd->bhqk', qi, ki) * intra[None]
        inner = jnp.einsum('bhqk,bhkv->bhqv', s, vi)
        cross = jnp.einsum('bhqd,bhdv->bhqv', qi, state) * q_decay[None, :, :, None]
        new_state = state * chunk_decay[None, :, None, None] + jnp.einsum(
            'bhkd,bhkv->bhdv', ki * k_decay[None, :, :, None], vi)
        return new_state, inner + cross

    state0 = jnp.zeros((B, H, dk, dv), f32)
    _, out = lax.scan(step, state0, (qc, kc, vc))
    out = out.transpose(1, 0, 3, 2, 4).reshape(B, S, H, dv)
    out = out * lax.rsqrt(jnp.mean(out * out, axis=-1, keepdims=True) + EPS)
    return out.astype(v.dtype)


def setup_inputs(seed: int = 0) -> dict:
    key = jax.random.key(seed)
    ks = jax.random.split(key, 16)
    f32 = jnp.float32

    def w(k, shape, fan_in):
        return jax.random.normal(k, shape, f32) * (fan_in ** -0.5)

    def gain(k):
        return 1.0 + 0.1 * jax.random.normal(k, (DEPTH, D_MODEL), f32)

    return {
        "x": jax.random.normal(ks[0], (BATCH, SEQ, D_MODEL), f32),
        "norm_ffn1_g": gain(ks[1]),
        "ffn1_w_gate": w(ks[2], (DEPTH, D_MODEL, D_FF), D_MODEL),
        "ffn1_w_up": w(ks[3], (DEPTH, D_MODEL, D_FF), D_MODEL),
        "ffn1_w_down": w(ks[4], (DEPTH, D_FF, D_MODEL), D_FF),
        "norm_mix_g": gain(ks[5]),
        "w_in": w(ks[6], (DEPTH, D_MODEL, IN_WIDTH), D_MODEL),
        "rel_bias": 0.5 * jax.random.normal(ks[7], (DEPTH, ATT_HEADS, N_REL), f32),
        "w_out_att": w(ks[8], (DEPTH, ATT_WIDTH, D_MODEL), ATT_WIDTH),
        "w_out_ret": w(ks[9], (DEPTH, RET_V_WIDTH, D_MODEL), RET_V_WIDTH),
        "w_out": w(ks[10], (DEPTH, D_MODEL, D_MODEL), D_MODEL),
        "norm_ffn2_g": gain(ks[11]),
        "ffn2_w_gate": w(ks[12], (DEPTH, D_MODEL, D_FF), D_MODEL),
        "ffn2_w_up": w(ks[13], (DEPTH, D_MODEL, D_FF), D_MODEL),
        "ffn2_w_down": w(ks[14], (DEPTH, D_FF, D_MODEL), D_FF),
        "norm_final_g": 1.0 + 0.1 * jax.random.normal(ks[15], (D_MODEL,), f32),
    }


def reference(x, norm_ffn1_g, ffn1_w_gate, ffn1_w_up, ffn1_w_down, norm_mix_g, w_in, rel_bias,
              w_out_att, w_out_ret, w_out, norm_ffn2_g, ffn2_w_gate, ffn2_w_up, ffn2_w_down,
              norm_final_g):
    B, S, _ = x.shape
    for l in range(DEPTH):
        x = x + 0.5 * swiglu_ffn(rmsnorm(x, norm_ffn1_g[l]), ffn1_w_gate[l], ffn1_w_up[l], ffn1_w_down[l])

        h = rmsnorm(x, norm_mix_g[l])
        proj = h @ w_in[l]
        q_a, k_a, v_a, q_r, k_r, v_r, g_r, gate_a, gate_r = jnp.split(proj, IN_SPLIT_POINTS, axis=-1)

        att = chunked_band_attention(
            q_a.reshape(B, S, ATT_HEADS, ATT_HEAD_DIM),
            k_a.reshape(B, S, ATT_HEADS, ATT_HEAD_DIM),
            v_a.reshape(B, S, ATT_HEADS, ATT_HEAD_DIM),
            rel_bias[l])
        branch_a = att @ w_out_att[l]

        ret = chunkwise_retention(
            rotary(q_r.reshape(B, S, RET_HEADS, RET_KEY_DIM)),
            rotary(k_r.reshape(B, S, RET_HEADS, RET_KEY_DIM)),
            v_r.reshape(B, S, RET_HEADS, RET_VAL_DIM))
        ret = jax.nn.silu(g_r) * ret.reshape(B, S, RET_V_WIDTH)
        branch_r = ret @ w_out_ret[l]

        merged = jax.nn.sigmoid(gate_a) * branch_a + jax.nn.sigmoid(gate_r) * branch_r
        x = x + merged @ w_out[l]

        x = x + 0.5 * swiglu_ffn(rmsnorm(x, norm_ffn2_g[l]), ffn2_w_gate[l], ffn2_w_up[l], ffn2_w_down[l])
    return rmsnorm(x, norm_final_g)
```

```python
import numpy as np
from contextlib import ExitStack
import concourse.bass as bass
import concourse.mybir as mybir
from concourse.bass_utils import run_bass_kernel_spmd

F32 = mybir.dt.float32
BF16 = mybir.dt.bfloat16
AF = mybir.ActivationFunctionType
ALU = mybir.AluOpType

D = 2048
DC = 16
DFF = 5632
FC = 44
T = 512
SEQ = 4096
HALF = 2048
NH = 8
EPS = 1e-6
NEG = -30000.0
WSLOTS = 4
POOL_INFLIGHT = 3
SLOT_ELEMS = 4096
QA, KA, VA, QR, KR, VR, GR, GA, GRT = 0, 1024, 2048, 3072, 4096, 5120, 6144, 7168, 9216


class Res:
    __slots__ = ("w", "r", "excl")

    def __init__(self, excl=False):
        self.w = None
        self.r = {}
        self.excl = excl


class Builder:
    def __init__(self, nc, es):
        self.nc = nc
        self.es = es
        self.eng = {"pe": nc.tensor, "act": nc.scalar, "dve": nc.vector, "pool": nc.gpsimd, "sp": nc.sync}
        self.semobj = {}
        self.cnt = {}
        self.waited = {e: {} for e in self.eng}
        for e in self.eng:
            self.semobj[e] = es.enter_context(nc.semaphore("s_" + e))
            self.cnt[e] = 0
        self.dma_sems = {"sp": [], "pool": []}
        for q in ("sp", "pool"):
            for i in range(8):
                k = "d%s%d" % (q, i)
                self.semobj[k] = es.enter_context(nc.semaphore("s_" + k))
                self.cnt[k] = 0
                self.dma_sems[q].append(k)
        self.dma_rr = {"sp": 0, "pool": 0}
        for k in ("cc0", "cc1"):
            self.semobj[k] = es.enter_context(nc.semaphore("s_" + k))
            self.cnt[k] = 0
        self.last_tok = {}
        self.bank_rr = 0
        self.bank_live = set()
        self.out_toks = []
        self.pool_hist = []

    def _wait(self, e, tok):
        key, val = tok
        if self.waited[e].get(key, 0) >= val:
            return
        self.eng[e].wait_ge(self.semobj[key], val)
        self.waited[e][key] = val

    def _deps(self, e, reads, writes):
        for r in reads:
            if r.w is not None:
                t = r.w
                if t[0] == e and e == "pe":
                    continue
                self._wait(e, t)
            if r.excl:
                for k, t in r.r.items():
                    if t[0] != e:
                        self._wait(e, t)
        for w in writes:
            if w.w is not None and not (w.w[0] == e and e == "pe"):
                self._wait(e, w.w)
            for k, t in w.r.items():
                if not (t[0] == e and e == "pe"):
                    self._wait(e, t)

    def _commit(self, tok, rkey, reads, writes):
        for r in reads:
            r.r[rkey] = tok
        for w in writes:
            w.w = tok
            w.r = {}

    def op(self, e, fn, reads=(), writes=()):
        self._deps(e, reads, writes)
        inst = fn()
        self.cnt[e] += 1
        inst.then_inc(self.semobj[e], 1)
        tok = (e, self.cnt[e])
        self._commit(tok, e, reads, writes)
        return tok

    def pe_group(self, fns, reads=(), writes=()):
        self._deps("pe", reads, writes)
        inst = None
        for fn in fns:
            inst = fn()
        self.cnt["pe"] += 1
        inst.then_inc(self.semobj["pe"], 1)
        tok = ("pe", self.cnt["pe"])
        self._commit(tok, "pe", reads, writes)
        return tok

    def dma(self, q, out, in_, reads=(), writes=(), semkey=None):
        if semkey is None:
            semkey = self.dma_sems[q][self.dma_rr[q] % len(self.dma_sems[q])]
            self.dma_rr[q] += 1
        prev = self.last_tok.get(semkey)
        if prev is not None:
            self._wait(q, prev)
        if q == "pool":
            self.pool_hist.append(None)
            if len(self.pool_hist) > POOL_INFLIGHT and self.pool_hist[-1 - POOL_INFLIGHT] is not None:
                self._wait(q, self.pool_hist[-1 - POOL_INFLIGHT])
        self._deps(q, reads, writes)
        inst = self.eng[q].dma_start(out=out, in_=in_)
        self.cnt[semkey] += 16
        inst.then_inc(self.semobj[semkey], 16)
        tok = (semkey, self.cnt[semkey])
        self.last_tok[semkey] = tok
        if q == "pool":
            self.pool_hist[-1] = tok
        self._commit(tok, semkey, reads, writes)
        return tok

    def collective(self, semkey, fn, reads=(), writes=()):
        q = "pool"
        self._deps(q, reads, writes)
        inst = fn()
        self.cnt[semkey] += 1
        inst.then_inc(self.semobj[semkey], 1)
        tok = (semkey, self.cnt[semkey])
        self._commit(tok, semkey, reads, writes)
        return tok

    def bank(self):
        for _ in range(8):
            b = self.bank_rr % 8
            self.bank_rr += 1
            if b not in self.bank_live:
                self.bank_live.add(b)
                return b
        raise RuntimeError("out of PSUM banks")

    def free(self, *bs):
        for b in bs:
            self.bank_live.discard(b)


class Rot:
    def __init__(self, n, name):
        self.n = n
        self.rr = 0
        self.live = set()
        self.name = name

    def get(self):
        for _ in range(self.n):
            i = self.rr % self.n
            self.rr += 1
            if i not in self.live:
                self.live.add(i)
                return i
        raise RuntimeError("out of " + self.name)

    def free(self, *idx):
        for i in idx:
            self.live.discard(i)


def build_program(n_pre=4, n_own=4, stop=None, mode="cc"):
    nc = bass.Bass("TRN2", target_bir_lowering=False)

    def din(name, shape):
        return nc.dram_tensor(name, list(shape), F32, kind="ExternalInput").ap()

    x_own = din("x_own", [D, HALF])
    x_pre = din("x_pre", [D, HALF]) if mode == "prefix" else None
    W = {}
    for nm, shp in [("ffn1_w_gate", (D, DFF)), ("ffn1_w_up", (D, DFF)), ("ffn1_w_down", (DFF, D)),
                    ("w_in", (D, 11264)), ("w_out_att", (1024, D)), ("w_out_ret", (1024, D)),
                    ("w_out", (D, D)), ("ffn2_w_gate", (D, DFF)), ("ffn2_w_up", (D, DFF)),
                    ("ffn2_w_down", (DFF, D))]:
        W[nm] = din(nm, shp)
    gains_d = din("gains", [128, 64])
    bias_d = din("bias_t", [128, NH * 3 * 128])
    cb_d = din("cb", [128, NH])
    pm_d = din("prevmask", [128, 1])
    cc_own_d = din("cc_own", [128, HALF])
    ss_own_d = din("ss_own", [128, HALF])
    cc_pre_d = din("cc_pre", [128, HALF]) if mode == "prefix" else None
    ss_pre_d = din("ss_pre", [128, HALF]) if mode == "prefix" else None
    flag_d = din("flag", [128, 1])
    x1s = nc.dram_tensor("x1s", [D, HALF], F32)
    st_in = nc.dram_tensor("st_in", [128, NH * 128], F32)
    st_out = nc.dram_tensor("st_out", [256, NH * 128], F32)
    kv_in = nc.dram_tensor("kv_in", [128, 8192], BF16)
    kv_out = nc.dram_tensor("kv_out", [256, 8192], BF16)
    krs = nc.dram_tensor("krs", [4 * NH * 128, T], BF16)
    kds = nc.dram_tensor("kds", [4 * NH * 128, T], BF16)
    vrs = nc.dram_tensor("vrs", [4 * 128, 4096], BF16)
    consts_d = din("consts", [128, 5 * 128])
    dt_d = din("dt_tab", [128, NH * 128])
    qd_d = din("qd_tab", [128, NH * 64])
    kd_d = din("kd_tab", [128, NH])
    y = nc.dram_tensor("y", [D, HALF], F32, kind="ExternalOutput").ap()

    with ExitStack() as es:
        B = Builder(nc, es)

        def sb(name, n, dt):
            return es.enter_context(nc.sbuf_tensor(name, [128, n], dt))

        xres = sb("xres", DC * T, F32)
        xn = sb("xn", DC * T, BF16)
        hb = sb("hb", 32 * T, BF16)
        wring = sb("wring", WSLOTS * SLOT_ELEMS, BF16)
        kc = sb("kc", NH * 2 * T, BF16)
        vc = sb("vc", 2 * 4 * 1024, BF16)
        biasT = sb("biasT", NH * 3 * 128, F32)
        cbt = sb("cbt", NH, F32)
        cbm = sb("cbm", NH, F32)
        pm = sb("pm", 1, F32)
        epsb = sb("epsb", 1, F32)
        flag = sb("flag_s", 1, F32)
        gains = sb("gains_s", 64, F32)
        kdt = sb("kdt", NH, F32)
        dtt = sb("dtt", NH * 128, F32)
        qdt = sb("qdt", NH * 64, F32)
        consts = sb("consts_s", 5 * 128, BF16)
        state = sb("state", NH * 128, F32)
        sbf0 = sb("sbf0", NH * 128, BF16)
        NTF = 8
        tmpf = sb("tmpf", NTF * T, F32)
        NTB = 10
        tmpb = sb("tmpb", NTB * T, BF16)
        tabs = sb("tabs", 2 * T, F32)
        kdtok = sb("kdtok", 2 * 512, BF16)
        pT = sb("pT", 2 * 640, BF16)
        psum = es.enter_context(nc.psum_tensor("psum", [128, 8 * 512], F32))

        R_xres = [Res() for _ in range(DC)]
        R_xn = [Res() for _ in range(DC)]
        R_hb = [Res() for _ in range(32)]
        R_slot = [Res() for _ in range(WSLOTS)]
        R_kc = [[Res() for _ in range(2)] for _ in range(NH)]
        R_vc = [[Res() for _ in range(4)] for _ in range(2)]
        R_bank = [Res(excl=True) for _ in range(8)]
        R_const = Res()
        R_state = [Res() for _ in range(NH)]
        R_sbf0 = [Res() for _ in range(NH)]
        R_tmpf = [Res() for _ in range(NTF)]
        R_tmpb = [Res() for _ in range(NTB)]
        R_tabs = Res()
        R_kdtok = [Res() for _ in range(2)]
        R_pT = [Res() for _ in range(2)]
        rr = {"slot": 0, "kd": 0, "pt": 0}
        R_krs = [[Res() for _ in range(NH)] for _ in range(4)]
        R_kds = [[Res() for _ in range(NH)] for _ in range(4)]
        R_vrs = [Res() for _ in range(4)]

        def krs_ap(t, h):
            return krs.ap()[(t * NH + h) * 128:(t * NH + h + 1) * 128, :]

        def kds_ap(t, h):
            return kds.ap()[(t * NH + h) * 128:(t * NH + h + 1) * 128, :]

        def vrs_ap(t):
            return vrs.ap()[t * 128:(t + 1) * 128, :]
        TF = Rot(NTF, "tmpf")
        TB = Rot(NTB, "tmpb")

        def xres_c(c):
            return xres[:, c * T:(c + 1) * T]

        def xn_c(c):
            return xn[:, c * T:(c + 1) * T]

        def hb_c(c):
            return hb[:, c * T:(c + 1) * T]

        def ps(b, n=512, off=0):
            return psum[:, b * 512 + off: b * 512 + off + n]

        def new_tf():
            i = TF.get()
            return i, tmpf[:, i * T:(i + 1) * T], R_tmpf[i]

        def new_tb():
            i = TB.get()
            return i, tmpb[:, i * T:(i + 1) * T], R_tmpb[i]

        ONES_D = consts[:, 0:128]
        ONES_HD = consts[:, 128:256]
        ONES = consts[:, 256:384]
        IDENT = consts[:, 384:512]
        PERMT = consts[:, 512:640]

        for dst, src in [(gains, gains_d), (biasT, bias_d), (cbt, cb_d), (pm, pm_d), (kdt, kd_d),
                         (dtt, dt_d), (qdt, qd_d), (flag, flag_d)]:
            B.dma("sp", dst[:], src, writes=[R_const])
        B.dma("pool", consts[:], consts_d, writes=[R_const])
        bt3 = biasT[:].rearrange("p (h j q) -> p h j q", h=NH, j=3)
        B.op("dve", lambda: nc.vector.memset(bt3[0:64, :, 0, 64:128], NEG), reads=[R_const], writes=[R_const])
        B.op("dve", lambda: nc.vector.memset(bt3[64:128, :, 2, 0:64], NEG), reads=[R_const], writes=[R_const])
        B.op("dve", lambda: nc.vector.tensor_scalar(cbm[:], cbt[:], pm[:, 0:1], None, ALU.add),
             reads=[R_const], writes=[R_const])
        B.op("dve", lambda: nc.vector.memset(epsb[:], EPS), reads=[R_const], writes=[R_const])
        B.op("dve", lambda: nc.vector.memset(state[:], 0.0), writes=R_state)
        B.op("dve", lambda: nc.vector.memset(sbf0[:], 0.0), writes=R_sbf0)
        B.op("dve", lambda: nc.vector.memset(kc[:], 0.0), writes=[r for rs in R_kc for r in rs])
        B.op("dve", lambda: nc.vector.memset(vc[:], 0.0), writes=[r for rs in R_vc for r in rs])

        def wload(wap, row0, kcn, col0, ncols):
            s = rr["slot"] % WSLOTS
            rr["slot"] += 1
            view = wring[:, s * SLOT_ELEMS: s * SLOT_ELEMS + kcn * ncols].rearrange("p (k n) -> p k n", k=kcn)
            src = wap[row0:row0 + kcn * 128, col0:col0 + ncols].rearrange("(k p) n -> p k n", p=128)
            B.dma("pool", view, src, writes=[R_slot[s]])
            return view, R_slot[s]

        def rmsnorm(gidx, final=False):
            bnk = B.bank()
            for c in range(DC):
                if c % 2 == 0:
                    B.op("act", lambda c=c: nc.scalar.activation(hb_c(c), xres_c(c), AF.Square),
                         reads=[R_xres[c]], writes=[R_hb[c]])
                else:
                    B.op("dve", lambda c=c: nc.vector.tensor_tensor(hb_c(c), xres_c(c), xres_c(c), ALU.mult),
                         reads=[R_xres[c]], writes=[R_hb[c]])
            B.pe_group([lambda c=c: nc.tensor.matmul(ps(bnk), ONES_D, hb_c(c), start=(c == 0), stop=(c == DC - 1))
                        for c in range(DC)], reads=R_hb[0:DC] + [R_const], writes=[R_bank[bnk]])
            i_sd, sd, r_sd = new_tf()
            B.op("act", lambda: nc.scalar.activation(sd, ps(bnk), AF.Ln, bias=epsb[:, 0:1], scale=1.0),
                 reads=[R_bank[bnk], R_const], writes=[r_sd])
            B.free(bnk)
            i_rs, rs, r_rs = new_tf()
            B.op("act", lambda: nc.scalar.activation(rs, sd, AF.Exp, scale=-0.5), reads=[r_sd], writes=[r_rs])
            TF.free(i_sd)
            for c in range(DC):
                gcol = gains[:, gidx * 16 + c: gidx * 16 + c + 1]
                if final:
                    B.op("dve", lambda c=c, gcol=gcol: nc.vector.scalar_tensor_tensor(
                        xres_c(c), xres_c(c), gcol, rs, ALU.mult, ALU.mult),
                        reads=[R_xres[c], r_rs, R_const], writes=[R_xres[c]])
                else:
                    B.op("dve", lambda c=c, gcol=gcol: nc.vector.scalar_tensor_tensor(
                        xn_c(c), xres_c(c), gcol, rs, ALU.mult, ALU.mult),
                        reads=[R_xres[c], r_rs, R_const], writes=[R_xn[c]])
            TF.free(i_rs)

        def ffn(pfx, gidx):
            wg, wu, wd = W[pfx + "_w_gate"], W[pfx + "_w_up"], W[pfx + "_w_down"]
            rmsnorm(gidx)
            for (c0, nch) in ((0, 24), (24, 20)):
                for gi in range(nch // 4):
                    col0 = (c0 + gi * 4) * 128
                    gs = [wload(wg, r * 1024, 8, col0, 512) for r in range(2)]
                    us = [wload(wu, r * 1024, 8, col0, 512) for r in range(2)]
                    tfs = []
                    for mm in range(4):
                        gb = B.bank()
                        B.pe_group([lambda k=k: nc.tensor.matmul(
                            ps(gb), gs[k // 8][0][:, k % 8, mm * 128:(mm + 1) * 128], xn_c(k),
                            start=(k == 0), stop=(k == 15)) for k in range(16)],
                            reads=[gs[0][1], gs[1][1]] + R_xn, writes=[R_bank[gb]])
                        i_tf, tf, r_tf = new_tf()
                        B.op("act", lambda tf=tf, gb=gb: nc.scalar.activation(tf, ps(gb), AF.Silu),
                             reads=[R_bank[gb]], writes=[r_tf])
                        B.free(gb)
                        tfs.append((i_tf, tf, r_tf))
                    for mm in range(4):
                        ub = B.bank()
                        B.pe_group([lambda k=k: nc.tensor.matmul(
                            ps(ub), us[k // 8][0][:, k % 8, mm * 128:(mm + 1) * 128], xn_c(k),
                            start=(k == 0), stop=(k == 15)) for k in range(16)],
                            reads=[us[0][1], us[1][1]] + R_xn, writes=[R_bank[ub]])
                        i_tf, tf, r_tf = tfs[mm]
                        j = gi * 4 + mm
                        B.op("dve", lambda tf=tf, ub=ub, j=j: nc.vector.tensor_tensor(hb_c(j), tf, ps(ub), ALU.mult),
                             reads=[r_tf, R_bank[ub]], writes=[R_hb[j]])
                        B.free(ub)
                        TF.free(i_tf)
                nblk = (nch + 7) // 8
                for cg in range(4):
                    banks = [B.bank() for _ in range(4)]
                    for bi in range(nblk):
                        kcn = min(8, nch - bi * 8)
                        sv, sr = wload(wd, (c0 + bi * 8) * 128, kcn, cg * 512, 512)
                        fns = []
                        for mm in range(4):
                            for kk in range(kcn):
                                k = bi * 8 + kk
                                fns.append(lambda mm=mm, kk=kk, k=k, sv=sv: nc.tensor.matmul(
                                    ps(banks[mm]), sv[:, kk, mm * 128:(mm + 1) * 128], hb_c(k),
                                    start=(k == 0), stop=(k == nch - 1)))
                        B.pe_group(fns, reads=[sr] + R_hb[bi * 8: bi * 8 + kcn], writes=[R_bank[b] for b in banks])
                    for mm in range(4):
                        m = cg * 4 + mm
                        B.op("dve", lambda m=m, mm=mm: nc.vector.scalar_tensor_tensor(
                            xres_c(m), ps(banks[mm]), 0.5, xres_c(m), ALU.mult, ALU.add),
                            reads=[R_bank[banks[mm]], R_xres[m]], writes=[R_xres[m]])
                    B.free(*banks)

        def proj_block(col0, ncols, consume):
            if ncols == 512:
                ws = [wload(W["w_in"], r * 1024, 8, col0, 512) for r in range(2)]
                lhs = lambda k, mm: ws[k // 8][0][:, k % 8, mm * 128:(mm + 1) * 128]
                rds = [ws[0][1], ws[1][1]]
            else:
                wv, wr_ = wload(W["w_in"], 0, 16, col0, ncols)
                lhs = lambda k, mm: wv[:, k, mm * 128:(mm + 1) * 128]
                rds = [wr_]
            for mm in range(ncols // 128):
                b = B.bank()
                B.pe_group([lambda k=k: nc.tensor.matmul(ps(b), lhs(k, mm), xn_c(k),
                                                         start=(k == 0), stop=(k == 15)) for k in range(16)],
                           reads=rds + R_xn, writes=[R_bank[b]])
                consume(mm, b)
                B.free(b)

        def proj_tokmajor(col0, consume):
            wv, wr_ = wload(W["w_in"], 0, 16, col0, 256)
            for pair in range(2):
                b = B.bank()
                fns = []
                for tb in range(2):
                    tbi = pair * 2 + tb
                    for k in range(16):
                        fns.append(lambda k=k, tb=tb, tbi=tbi: nc.tensor.matmul(
                            ps(b, 256, tb * 256), xn[:, k * T + tbi * 128: k * T + (tbi + 1) * 128], wv[:, k, :],
                            start=(k == 0), stop=(k == 15)))
                B.pe_group(fns, reads=[wr_] + R_xn, writes=[R_bank[b]])
                consume(pair, b)
                B.free(b)

        def load_tabs(cc_d, ss_d, tok0):
            B.dma("sp", tabs[:, 0:T], cc_d[:, tok0:tok0 + T], writes=[R_tabs])
            B.dma("sp", tabs[:, T:2 * T], ss_d[:, tok0:tok0 + T], writes=[R_tabs])

        def rotary_from_psum(b, dst_ap, dst_res):
            i_xb, xb, r_xb = new_tb()
            B.op("act", lambda: nc.scalar.activation(xb, ps(b), AF.Copy), reads=[R_bank[b]], writes=[r_xb])
            b2 = B.bank()
            B.pe_group([lambda: nc.tensor.matmul(ps(b2), PERMT, xb, start=True, stop=True)],
                       reads=[r_xb, R_const], writes=[R_bank[b2]])
            i1, t1, r_t1 = new_tf()
            B.op("dve", lambda: nc.vector.tensor_tensor(t1, ps(b), tabs[:, 0:T], ALU.mult),
                 reads=[R_bank[b], R_tabs], writes=[r_t1])
            i2, t2, r_t2 = new_tf()
            B.op("dve", lambda: nc.vector.tensor_tensor(t2, ps(b2), tabs[:, T:2 * T], ALU.mult),
                 reads=[R_bank[b2], R_tabs], writes=[r_t2])
            B.op("dve", lambda: nc.vector.tensor_tensor(dst_ap, t1, t2, ALU.add),
                 reads=[r_t1, r_t2], writes=[dst_res])
            B.free(b2)
            TB.free(i_xb)
            TF.free(i1, i2)

        VR0 = 16
        SH0 = 24

        def vr_ap(blk, col0, n, p0=0, pn=128):
            base = VR0 * T + blk * 1024 + col0
            return hb[p0:p0 + pn, base: base + n]

        shf = hb[:, SH0 * T:(SH0 + 4) * T].bitcast(F32)
        shb_inst = [hb[:, (SH0 + 4) * T:(SH0 + 6) * T], hb[:, (SH0 + 6) * T:(SH0 + 8) * T]]
        R_vr = R_hb[VR0:VR0 + 8]
        R_shf = R_hb[SH0:SH0 + 4]
        R_shb_inst = [R_hb[SH0 + 4:SH0 + 6], R_hb[SH0 + 6:SH0 + 8]]

        def retention_state_update(h, krT, r_krT, want_hist, t=None, spill=False, preload=False):
            shb, R_shb = shb_inst[h % 2], R_shb_inst[h % 2]
            i = rr["kd"] % 2
            rr["kd"] += 1
            kdv = kdtok[:, i * 512:(i + 1) * 512]
            if preload:
                B.dma("sp", kdv, kds_ap(t, h), reads=[R_kds[t][h]], writes=[R_kdtok[i]])
            else:
                bt = B.bank()
                B.pe_group([lambda blk=blk: nc.tensor.matmul(ps(bt, 128, blk * 128), krT[:, blk * 128:(blk + 1) * 128],
                                                             IDENT, start=True, stop=True) for blk in range(4)],
                           reads=[r_krT, R_const], writes=[R_bank[bt]])
                B.op("act", lambda: nc.scalar.activation(kdv, ps(bt), AF.Copy, scale=kdt[:, h:h + 1]),
                     reads=[R_bank[bt], R_const], writes=[R_kdtok[i]])
                B.free(bt)
                if spill:
                    B.dma("sp", kds_ap(t, h), kdv, reads=[R_kdtok[i]], writes=[R_kds[t][h]])
            yield
            ub = [B.bank(), B.bank()]
            for g in range(2):
                fns = []
                hp = g * 64
                for cc in range(4):
                    blk = cc
                    fns.append(lambda cc=cc, blk=blk, hp=hp, g=g: nc.tensor.matmul(
                        ps(ub[g], 128, cc * 128),
                        kdtok[hp:hp + 64, i * 512 + blk * 128: i * 512 + (blk + 1) * 128],
                        vr_ap(blk, h * 128, 128, hp, 64), start=True, stop=True))
                B.pe_group(fns, reads=[R_kdtok[i]] + R_vr, writes=[R_bank[ub[g]]])
            yield
            cd = float((1.0 - 2.0 ** (-5.0 - h)) ** 64)
            st = state[:, h * 128:(h + 1) * 128]
            for c in range(8):
                g, cc = c % 2, c // 2
                dst = shf[:, c * 128:(c + 1) * 128]
                src = st if c == 0 else shf[:, (c - 1) * 128: c * 128]
                B.op("dve", lambda dst=dst, src=src, g=g, cc=cc: nc.vector.scalar_tensor_tensor(
                    dst, src, cd, ps(ub[g], 128, cc * 128), ALU.mult, ALU.add),
                    reads=[R_state[h], R_bank[ub[g]]] + R_shf, writes=R_shf)
            B.free(*ub)
            if want_hist:
                B.op("act", lambda: nc.scalar.activation(shb[:, 128:1024], shf[:, 0:896], AF.Copy),
                     reads=R_shf, writes=R_shb)
                B.op("act", lambda: nc.scalar.activation(shb[:, 0:128], sbf0[:, h * 128:(h + 1) * 128], AF.Copy),
                     reads=[R_sbf0[h]] + R_shb, writes=R_shb)
            B.op("act", lambda: nc.scalar.activation(st, shf[:, 896:1024], AF.Copy),
                 reads=R_shf, writes=[R_state[h]])
            B.op("act", lambda: nc.scalar.activation(sbf0[:, h * 128:(h + 1) * 128], shf[:, 896:1024], AF.Copy),
                 reads=R_shf, writes=[R_sbf0[h]])

        def load_vr():
            for cb2 in range(4):
                def consume(pair, b, cb2=cb2):
                    for tb in range(2):
                        blk = pair * 2 + tb
                        B.op("act", lambda tb=tb, blk=blk: nc.scalar.activation(
                            vr_ap(blk, cb2 * 256, 256), ps(b, 256, tb * 256), AF.Copy),
                            reads=[R_bank[b]], writes=R_vr)
                proj_tokmajor(VR + cb2 * 256, consume)

        def k_att_block4(cb4, hf):
            def consume(mm, b):
                h = cb4 * 4 + mm
                B.op("act", lambda: nc.scalar.activation(
                    kc[:, (h * 2 + hf) * T:(h * 2 + hf + 1) * T], ps(b), AF.Copy),
                    reads=[R_bank[b]], writes=[R_kc[h][hf]])
            proj_block(KA + cb4 * 512, 512, consume)

        def k_att_block(cb2, hf):
            if cb2 % 2 == 0:
                k_att_block4(cb2 // 2, hf)

        def v_att_block(cb2, hf):
            def consume(pair, b):
                for tb in range(2):
                    blk = pair * 2 + tb
                    B.op("act", lambda tb=tb, blk=blk: nc.scalar.activation(
                        vc[:, (hf * 4 + blk) * 1024 + cb2 * 256:(hf * 4 + blk) * 1024 + (cb2 + 1) * 256],
                        ps(b, 256, tb * 256), AF.Copy),
                        reads=[R_bank[b]], writes=[R_vc[hf][blk]])
            proj_tokmajor(VA + cb2 * 256, consume)

        def attention_head(h, hf, qT, r_qT, first_tile):
            scale = float(128 ** -0.5)
            ob = B.bank()
            db = B.bank()
            pidx = {0: 0, 3: 1, 4: 2, 1: 3, 2: 4}
            st_ = {}

            def scores(pr):
                bx, by = B.bank(), B.bank()
                pos = {0: (bx, 0), 3: (bx, 128), 4: (bx, 256), 1: (by, 0), 2: (by, 128)}
                blocks = {}
                fx, fy = [], []
                rd = [r_qT]
                for j in range(5):
                    bi = pr - 4 + j
                    if bi < 0:
                        hh, blk = 1 - hf, 4 + bi
                    else:
                        hh, blk = hf, bi
                    blocks[j] = (hh, blk)
                    kap = kc[:, (h * 2 + hh) * T + blk * 128:(h * 2 + hh) * T + (blk + 1) * 128]
                    bnk, off = pos[j]
                    f = (lambda kap=kap, bnk=bnk, off=off: nc.tensor.matmul(
                        ps(bnk, 128, off), kap, qT[:, pr * 128:(pr + 1) * 128], start=True, stop=True))
                    (fx if bnk == bx else fy).append(f)
                    if R_kc[h][hh] not in rd:
                        rd.append(R_kc[h][hh])
                B.pe_group(fx, reads=rd, writes=[R_bank[bx]])
                B.pe_group(fy, reads=rd, writes=[R_bank[by]])
                st_[pr] = (bx, by, blocks)

            def expo(pr):
                bx, by, blocks = st_[pr]
                pi = rr["pt"] % 2
                rr["pt"] += 1
                pbuf = pT[:, pi * 640:(pi + 1) * 640]
                i_tf, tf, r_tf = new_tf()
                B.op("dve", lambda: nc.vector.scalar_tensor_tensor(
                    tf[:, 0:384], ps(bx, 384), scale, biasT[:, h * 384:(h + 1) * 384], ALU.mult, ALU.add),
                    reads=[R_bank[bx], R_const], writes=[r_tf])
                if not first_tile:
                    B.op("act", lambda: nc.scalar.activation(pbuf[:, 0:384], tf[:, 0:384], AF.Exp),
                         reads=[r_tf], writes=[R_pT[pi]])
                    B.op("act", lambda: nc.scalar.activation(pbuf[:, 384:640], ps(by, 256), AF.Exp,
                                                             bias=cbt[:, h:h + 1], scale=scale),
                         reads=[R_bank[by], R_const], writes=[R_pT[pi]])
                else:
                    for j in (0, 3, 4):
                        c0 = pidx[j] * 128
                        if pr - 4 + j < 0:
                            B.op("act", lambda c0=c0: nc.scalar.activation(
                                pbuf[:, c0:c0 + 128], tf[:, c0:c0 + 128], AF.Exp, bias=pm[:, 0:1], scale=1.0),
                                reads=[r_tf, R_const], writes=[R_pT[pi]])
                        else:
                            B.op("act", lambda c0=c0: nc.scalar.activation(
                                pbuf[:, c0:c0 + 128], tf[:, c0:c0 + 128], AF.Exp),
                                reads=[r_tf], writes=[R_pT[pi]])
                    for j in (1, 2):
                        c0 = pidx[j] * 128
                        bcol = (cbm if (pr - 4 + j < 0) else cbt)[:, h:h + 1]
                        B.op("act", lambda c0=c0, bcol=bcol: nc.scalar.activation(
                            pbuf[:, c0:c0 + 128], ps(by, 128, c0 - 384), AF.Exp, bias=bcol, scale=scale),
                            reads=[R_bank[by], R_const], writes=[R_pT[pi]])
                B.free(bx, by)
                TF.free(i_tf)
                st_[pr] = (pi, pbuf, blocks)

            def pv(pr):
                pi, pbuf, blocks = st_[pr]
                fo, fd = [], []
                rdv = [R_pT[pi], R_const]
                for n_, j in enumerate((0, 3, 4, 1, 2)):
                    hh, blk = blocks[j]
                    vap = vc[:, (hh * 4 + blk) * 1024 + h * 128:(hh * 4 + blk) * 1024 + (h + 1) * 128]
                    pj = pbuf[:, pidx[j] * 128:(pidx[j] + 1) * 128]
                    fo.append(lambda vap=vap, pj=pj, n_=n_: nc.tensor.matmul(
                        ps(ob, 128, pr * 128), vap, pj, start=(n_ == 0), stop=(n_ == 4)))
                    fd.append(lambda pj=pj, n_=n_: nc.tensor.matmul(
                        ps(db, 128, pr * 128), ONES, pj, start=(n_ == 0), stop=(n_ == 4)))
                    if R_vc[hh][blk] not in rdv:
                        rdv.append(R_vc[hh][blk])
                B.pe_group(fo + fd, reads=rdv, writes=[R_bank[ob], R_bank[db]])

            scores(0)
            for pr in range(4):
                if pr + 1 < 4:
                    scores(pr + 1)
                expo(pr)
                pv(pr)
            i_rd, rd_, r_rd = new_tf()
            B.op("act", lambda: nc.scalar.activation(rd_, ps(db), AF.Ln), reads=[R_bank[db]], writes=[r_rd])
            B.op("act", lambda: nc.scalar.activation(rd_, rd_, AF.Exp, scale=-1.0), reads=[r_rd], writes=[r_rd])
            B.op("dve", lambda: nc.vector.tensor_tensor(hb_c(h), ps(ob), rd_, ALU.mult),
                 reads=[R_bank[ob], r_rd], writes=[R_hb[h]])
            B.free(ob, db)
            TF.free(i_rd)

        def retention_head(h, lite, t=None, spill=False, preload=False):
            shb, R_shb = shb_inst[h % 2], R_shb_inst[h % 2]
            res = {}

            def consume_k(mm, b):
                i_k, krT, r_krT = new_tb()
                rotary_from_psum(b, krT, r_krT)
                res["k"] = (i_k, krT, r_krT)

            if preload:
                i_k, krT, r_krT = new_tb()
                B.dma("sp", krT, krs_ap(t, h), reads=[R_krs[t][h]], writes=[r_krT])
            else:
                proj_block(KR + h * 128, 128, consume_k)
                i_k, krT, r_krT = res["k"]
                if spill:
                    B.dma("sp", krs_ap(t, h), krT, reads=[r_krT], writes=[R_krs[t][h]])
            yield
            if lite:
                yield from retention_state_update(h, krT, r_krT, want_hist=False, t=t, spill=spill)
                TB.free(i_k)
                return

            def consume_q(mm, b):
                i_q, qrT, r_qrT = new_tb()
                i_qd, qdT, r_qdT = new_tb()
                rotary_from_psum(b, qrT, r_qrT)
                qd3 = bass.AP(qdt, h * 64, [[NH * 64, 128], [0, 8], [1, 64]])
                B.op("dve", lambda: nc.vector.tensor_tensor(
                    qdT.rearrange("p (a b) -> p a b", a=8), qrT.rearrange("p (a b) -> p a b", a=8), qd3, ALU.mult),
                    reads=[r_qrT, R_const], writes=[r_qdT])
                res["q"] = (i_q, qrT, r_qrT, i_qd, qdT, r_qdT)

            proj_block(QR + h * 128, 128, consume_q)
            i_q, qrT, r_qrT, i_qd, qdT, r_qdT = res["q"]
            yield
            yield from retention_state_update(h, krT, r_krT, want_hist=True, t=t, preload=preload)
            yield
            sb_ = B.bank()
            B.pe_group([lambda pr=pr: nc.tensor.matmul(ps(sb_, 128, pr * 128), krT[:, pr * 128:(pr + 1) * 128],
                                                       qrT[:, pr * 128:(pr + 1) * 128], start=True, stop=True)
                        for pr in range(4)], reads=[r_krT, r_qrT], writes=[R_bank[sb_]])
            i_sm, sm, r_sm = new_tb()
            dt3 = bass.AP(dtt, h * 128, [[NH * 128, 128], [0, 4], [1, 128]])
            B.op("dve", lambda: nc.vector.tensor_tensor(
                sm.rearrange("p (a b) -> p a b", a=4), ps(sb_).rearrange("p (a b) -> p a b", a=4), dt3, ALU.mult),
                reads=[R_bank[sb_], R_const], writes=[r_sm])
            B.free(sb_)
            yield
            ob = B.bank()
            fns = []
            for pr in range(4):
                vap = vr_ap(pr, h * 128, 128)
                fns.append(lambda pr=pr, vap=vap: nc.tensor.matmul(
                    ps(ob, 128, pr * 128), vap, sm[:, pr * 128:(pr + 1) * 128], start=True, stop=False,
                    skip_group_check=True))
                for cc in range(2):
                    c = pr * 2 + cc
                    fns.append(lambda c=c, cc=cc: nc.tensor.matmul(
                        ps(ob, 64, c * 64), shb[:, c * 128:(c + 1) * 128], qdT[:, c * 64:(c + 1) * 64],
                        start=False, stop=True, skip_group_check=True))
            B.pe_group(fns, reads=R_vr + [r_sm, r_qdT] + R_shb, writes=[R_bank[ob]])
            TB.free(i_k, i_q, i_qd, i_sm)
            i_o, osb, r_osb = new_tf()
            B.op("act", lambda: nc.scalar.activation(osb, ps(ob), AF.Copy), reads=[R_bank[ob]], writes=[r_osb])
            i_sq, sq, r_sq = new_tb()
            B.op("act", lambda: nc.scalar.activation(sq, ps(ob), AF.Square), reads=[R_bank[ob]], writes=[r_sq])
            B.free(ob)
            yield
            nb = B.bank()
            B.pe_group([lambda: nc.tensor.matmul(ps(nb), ONES_HD, sq, start=True, stop=True)],
                       reads=[r_sq, R_const], writes=[R_bank[nb]])
            TB.free(i_sq)
            i_sd, sd, r_sd = new_tf()
            B.op("act", lambda: nc.scalar.activation(sd, ps(nb), AF.Ln, bias=epsb[:, 0:1], scale=1.0),
                 reads=[R_bank[nb], R_const], writes=[r_sd])
            B.free(nb)
            i_rs, rs, r_rs = new_tf()
            B.op("act", lambda: nc.scalar.activation(rs, sd, AF.Exp, scale=-0.5), reads=[r_sd], writes=[r_rs])
            TF.free(i_sd)
            i_on, on, r_on = new_tf()
            B.op("dve", lambda: nc.vector.tensor_tensor(on, osb, rs, ALU.mult), reads=[r_osb, r_rs], writes=[r_on])
            TF.free(i_o, i_rs)
            yield

            B.op("dve", lambda: nc.vector.scalar_tensor_tensor(hb_c(8 + h), on, 0.5, hb_c(8 + h), ALU.mult, ALU.mult),
                 reads=[r_on, R_hb[8 + h]], writes=[R_hb[8 + h]])
            TF.free(i_on)

        def gate_r_precompute():
            for cb4 in range(2):
                def consume_g(mm, b, cb4=cb4):
                    h = cb4 * 4 + mm
                    i_th, th, r_th = new_tf()
                    B.op("act", lambda: nc.scalar.activation(th, ps(b), AF.Tanh, scale=0.5),
                         reads=[R_bank[b]], writes=[r_th])
                    B.op("dve", lambda: nc.vector.scalar_tensor_tensor(hb_c(8 + h), th, 1.0, ps(b), ALU.add, ALU.mult),
                         reads=[r_th, R_bank[b]], writes=[R_hb[8 + h]])
                    TF.free(i_th)
                proj_block(GR + cb4 * 512, 512, consume_g)

        def run_interleaved(gens, width=2, stagger=4):
            active = []
            it = iter(gens)
            exhausted = False
            while True:
                if not exhausted and len(active) < width:
                    g = next(it, None)
                    if g is None:
                        exhausted = True
                    else:
                        if active:
                            for _ in range(stagger):
                                for a in list(active):
                                    try:
                                        next(a)
                                    except StopIteration:
                                        active.remove(a)
                        active.append(g)
                        continue
                if not active:
                    if exhausted:
                        break
                    continue
                for g in list(active):
                    try:
                        next(g)
                    except StopIteration:
                        active.remove(g)

        def merge_and_out():
            for g4 in range(4):
                c0 = g4 * 512
                gas = [wload(W["w_in"], r * 1024, 8, GA + c0, 512) for r in range(2)]
                av, ar_ = wload(W["w_out_att"], 0, 8, c0, 512)
                tas = []
                for mm in range(4):
                    cs = slice(mm * 128, (mm + 1) * 128)
                    ga = B.bank()
                    B.pe_group([lambda k=k: nc.tensor.matmul(ps(ga), gas[k // 8][0][:, k % 8, cs], xn_c(k),
                                                             start=(k == 0), stop=(k == 15)) for k in range(16)],
                               reads=[gas[0][1], gas[1][1]] + R_xn, writes=[R_bank[ga]])
                    i_ta, ta, r_ta = new_tf()
                    B.op("act", lambda ta=ta, ga=ga: nc.scalar.activation(ta, ps(ga), AF.Tanh, scale=0.5),
                         reads=[R_bank[ga]], writes=[r_ta])
                    B.free(ga)
                    tas.append((i_ta, ta, r_ta))
                for mm in range(4):
                    cs = slice(mm * 128, (mm + 1) * 128)
                    ba = B.bank()
                    B.pe_group([lambda k=k: nc.tensor.matmul(ps(ba), av[:, k, cs], hb_c(k), start=(k == 0), stop=(k == 7))
                                for k in range(8)], reads=[ar_] + R_hb[0:8], writes=[R_bank[ba]])
                    i_ta, ta, r_ta = tas[mm]
                    B.op("dve", lambda ta=ta, ba=ba: nc.vector.scalar_tensor_tensor(ta, ta, 1.0, ps(ba), ALU.add, ALU.mult),
                         reads=[r_ta, R_bank[ba]], writes=[r_ta])
                    B.free(ba)
                grs = [wload(W["w_in"], r * 1024, 8, GRT + c0, 512) for r in range(2)]
                rv, rr_ = wload(W["w_out_ret"], 0, 8, c0, 512)
                trs = []
                for mm in range(4):
                    cs = slice(mm * 128, (mm + 1) * 128)
                    gr = B.bank()
                    B.pe_group([lambda k=k: nc.tensor.matmul(ps(gr), grs[k // 8][0][:, k % 8, cs], xn_c(k),
                                                             start=(k == 0), stop=(k == 15)) for k in range(16)],
                               reads=[grs[0][1], grs[1][1]] + R_xn, writes=[R_bank[gr]])
                    i_tr, tr, r_tr = new_tf()
                    B.op("act", lambda tr=tr, gr=gr: nc.scalar.activation(tr, ps(gr), AF.Tanh, scale=0.5),
                         reads=[R_bank[gr]], writes=[r_tr])
                    B.free(gr)
                    trs.append((i_tr, tr, r_tr))
                for mm in range(4):
                    cs = slice(mm * 128, (mm + 1) * 128)
                    m = g4 * 4 + mm
                    br = B.bank()
                    B.pe_group([lambda k=k: nc.tensor.matmul(ps(br), rv[:, k, cs], hb_c(8 + k), start=(k == 0), stop=(k == 7))
                                for k in range(8)], reads=[rr_] + R_hb[8:16], writes=[R_bank[br]])
                    i_tr, tr, r_tr = trs[mm]
                    i_ta, ta, r_ta = tas[mm]
                    B.op("dve", lambda tr=tr, br=br: nc.vector.scalar_tensor_tensor(tr, tr, 1.0, ps(br), ALU.add, ALU.mult),
                         reads=[r_tr, R_bank[br]], writes=[r_tr])
                    B.free(br)
                    B.op("dve", lambda m=m, ta=ta, tr=tr: nc.vector.tensor_tensor(hb_c(16 + m), ta, tr, ALU.add),
                         reads=[r_ta, r_tr], writes=[R_hb[16 + m]])
                    TF.free(i_ta, i_tr)
            for g4 in range(4):
                ovs = [wload(W["w_out"], r * 1024, 8, g4 * 512, 512) for r in range(2)]
                for mm in range(4):
                    m = g4 * 4 + mm
                    cs = slice(mm * 128, (mm + 1) * 128)
                    b = B.bank()
                    B.pe_group([lambda k=k: nc.tensor.matmul(ps(b), ovs[k // 8][0][:, k % 8, cs], hb_c(16 + k),
                                                             start=(k == 0), stop=(k == 15)) for k in range(16)],
                               reads=[ovs[0][1], ovs[1][1]] + R_hb[16:32], writes=[R_bank[b]])
                    B.op("dve", lambda m=m, b=b: nc.vector.scalar_tensor_tensor(
                        xres_c(m), ps(b), 0.5, xres_c(m), ALU.mult, ALU.add),
                        reads=[R_bank[b], R_xres[m]], writes=[R_xres[m]])
                    B.free(b)

        def xfer_x(dram_ap, tok0, to_sbuf, extra_reads=(), extra_writes=()):
            toks = []
            for g in range(4):
                sb_ap = xres[:, g * 4 * T:(g + 1) * 4 * T].rearrange("p (c t) -> p c t", c=4)
                dr_ap = dram_ap[g * 512:(g + 1) * 512, tok0:tok0 + T].rearrange("(c p) t -> p c t", p=128)
                if to_sbuf:
                    toks.append(B.dma("sp", sb_ap, dr_ap, reads=list(extra_reads), writes=R_xres[g * 4:(g + 1) * 4]))
                else:
                    toks.append(B.dma("sp", dr_ap, sb_ap, reads=R_xres[g * 4:(g + 1) * 4], writes=list(extra_writes)))
            return toks

        def load_x(src, tok0):
            xfer_x(src, tok0, True)

        def x_tile_ap(dram_ap, t):
            return dram_ap[:, t * T:(t + 1) * T].rearrange("(c p) t -> p c t", p=128)

        xres3 = xres[:].rearrange("p (c t) -> p c t", c=DC)

        def own_tile_phase2(t, after_kv=None):
            hf = t % 2
            for cb2 in range(4):
                k_att_block(cb2, hf)
                v_att_block(cb2, hf)
            if after_kv is not None:
                after_kv()
            for cb4 in range(2):
                def consume_q(mm, b, cb4=cb4, t=t, hf=hf):
                    h = cb4 * 4 + mm
                    i_q, qT, r_qT = new_tb()
                    B.op("act", lambda: nc.scalar.activation(qT, ps(b), AF.Copy), reads=[R_bank[b]], writes=[r_qT])
                    attention_head(h, hf, qT, r_qT, first_tile=(t == 0))
                    TB.free(i_q)
                proj_block(QA + cb4 * 512, 512, consume_q)
            gate_r_precompute()
            if mode == "cc":
                B.dma("sp", hb[:, VR0 * T:VR0 * T + 4096], vrs_ap(t), reads=[R_vrs[t]], writes=R_vr)
                run_interleaved([retention_head(h, lite=False, t=t, preload=True) for h in range(NH)])
            else:
                load_vr()
                run_interleaved([retention_head(h, lite=False) for h in range(NH)])
            merge_and_out()
            ffn("ffn2", 2)
            rmsnorm(3, final=True)
            B.out_toks.extend(xfer_x(y, t * T, False))

        if mode == "cc":
            R_x1s = [Res() for _ in range(4)]
            R_stin, R_stout, R_kvin, R_kvout = Res(), Res(), Res(), Res()
            for t in range(n_own):
                hf = t % 2
                load_x(x_own, t * T)
                load_tabs(cc_own_d, ss_own_d, t * T)
                ffn("ffn1", 0)
                xfer_x(x1s.ap(), t * T, False, extra_writes=[R_x1s[t]])
                rmsnorm(1)
                if t == n_own - 1:
                    for cb2 in range(4):
                        k_att_block(cb2, 1)
                        v_att_block(cb2, 1)
                load_vr()
                B.dma("sp", vrs_ap(t), hb[:, VR0 * T:VR0 * T + 4096], reads=R_vr, writes=[R_vrs[t]])
                run_interleaved([retention_head(h, lite=True, t=t, spill=True) for h in range(NH)])
            kc4 = kc[:].rearrange("p (h f t) -> p h f t", h=NH, f=2)
            B.dma("sp", st_in.ap(), state[:], reads=R_state, writes=[R_stin])
            B.dma("sp", kv_in.ap()[:, 0:4096].rearrange("p (h t) -> p h t", h=NH), kc4[:, :, 1, :],
                  reads=[R_kc[h][1] for h in range(NH)], writes=[R_kvin])
            B.dma("sp", kv_in.ap()[:, 4096:8192], vc[:, 4096:8192], reads=R_vc[1], writes=[R_kvin])
            groups = [[0, 1], [2, 3], [4, 5], [6, 7]]
            B.collective("cc0", lambda: nc.gpsimd.collective_compute(
                "AllGather", ALU.bypass, replica_groups=groups, ins=[st_in.ap().opt()], outs=[st_out.ap().opt()]),
                reads=[R_stin], writes=[R_stout])
            B.collective("cc1", lambda: nc.gpsimd.collective_compute(
                "AllGather", ALU.bypass, replica_groups=groups, ins=[kv_in.ap().opt()], outs=[kv_out.ap().opt()]),
                reads=[R_kvin], writes=[R_kvout])
            def do_import():
                B.dma("sp", state[:], st_out.ap()[0:128, :], reads=[R_stout], writes=R_state)
                B.dma("sp", kc4[:, :, 1, :], kv_out.ap()[0:128, 0:4096].rearrange("p (h t) -> p h t", h=NH),
                      reads=[R_kvout], writes=[R_kc[h][1] for h in range(NH)])
                B.dma("sp", vc[:, 4096:8192], kv_out.ap()[0:128, 4096:8192], reads=[R_kvout], writes=R_vc[1])
                B.op("dve", lambda: nc.vector.tensor_scalar(state[:], state[:], flag[:, 0:1], None, ALU.mult),
                     reads=R_state + [R_const], writes=R_state)
                B.op("act", lambda: nc.scalar.activation(sbf0[:], state[:], AF.Copy), reads=R_state, writes=R_sbf0)
            for t in range(n_own):
                xfer_x(x1s.ap(), t * T, True, extra_reads=[R_x1s[t]])
                load_tabs(cc_own_d, ss_own_d, t * T)
                rmsnorm(1)
                own_tile_phase2(t, after_kv=(do_import if t == 0 else None))
            for tok in B.out_toks:
                B._wait("sp", tok)
            return nc

        for t in range(4 - n_pre, 4):
            hf = t % 2
            load_x(x_pre, t * T)
            load_tabs(cc_pre_d, ss_pre_d, t * T)
            ffn("ffn1", 0)
            rmsnorm(1)
            if t == 3:
                for cb2 in range(4):
                    k_att_block(cb2, hf)
                    v_att_block(cb2, hf)
            load_vr()
            run_interleaved([retention_head(h, lite=True) for h in range(NH)])

        class _Stop(Exception):
            pass

        def finish(t):
            tok = B.dma("sp", y[:, t * T:(t + 1) * T].rearrange("(c p) t -> p c t", p=128),
                        xres[:].rearrange("p (c t) -> p c t", c=DC), reads=R_xres)
            B.out_toks.append(tok)
            for tok in B.out_toks:
                B._wait("sp", tok)

        for t in range(n_own):
            hf = t % 2
            load_x(x_own, t * T)
            if stop == "load":
                finish(t)
                return nc
            load_tabs(cc_own_d, ss_own_d, t * T)
            if stop == "norm0":
                rmsnorm(0, final=True)
                finish(t)
                return nc
            ffn("ffn1", 0)
            if stop == "ffn1":
                finish(t)
                return nc
            rmsnorm(1)
            if stop == "vr2":
                load_vr()
                finish(t)
                return nc
            if stop == "vr4":
                for cb2 in range(4):
                    k_att_block(cb2, hf)
                    v_att_block(cb2, hf)
                for cb2 in range(4):
                    k_att_block(cb2, hf)
                    v_att_block(cb2, hf)
                finish(t)
                return nc
            if stop == "vr6":
                ffn("ffn1", 0)
                finish(t)
                return nc
            if stop == "vr3":
                for cb2 in range(4):
                    k_att_block(cb2, hf)
                    v_att_block(cb2, hf)
                for cb2 in range(4):
                    k_att_block(cb2, hf)
                    v_att_block(cb2, hf)
                for cb2 in range(4):
                    k_att_block(cb2, hf)
                    v_att_block(cb2, hf)
                finish(t)
                return nc
            for cb2 in range(4):
                k_att_block(cb2, hf)
                v_att_block(cb2, hf)
            for cb2 in range(4):
                def consume_q(mm, b, cb2=cb2, t=t, hf=hf):
                    h = cb2 * 2 + mm
                    i_q, qT, r_qT = new_tb()
                    B.op("act", lambda: nc.scalar.activation(qT, ps(b), AF.Copy), reads=[R_bank[b]], writes=[r_qT])
                    attention_head(h, hf, qT, r_qT, first_tile=(t == 0))
                    TB.free(i_q)
                proj_block(QA + cb2 * 256, 256, consume_q)
            if stop == "att":
                finish(t)
                return nc
            gate_r_precompute()
            load_vr()
            if stop == "vr":
                finish(t)
                return nc
            if stop in ("rl", "rl1", "rl2", "rl3"):
                run_interleaved([retention_head(0, lite=True)])
                finish(t)
                return nc
            if stop == "r0":
                run_interleaved([retention_head(0, lite=False)])
                finish(t)
                return nc
            run_interleaved([retention_head(h, lite=False) for h in range(NH)])
            if stop == "ret":
                finish(t)
                return nc
            merge_and_out()
            ffn("ffn2", 2)
            rmsnorm(3, final=True)
            tok = B.dma("sp", y[:, t * T:(t + 1) * T].rearrange("(c p) t -> p c t", p=128),
                        xres[:].rearrange("p (c t) -> p c t", c=DC), reads=R_xres)
            B.out_toks.append(tok)
        for tok in B.out_toks:
            B._wait("sp", tok)
    return nc


def _host_tables():
    inv = 1.0 / (10000.0 ** (np.arange(0, 128, 2, dtype=np.float64) / 128.0))
    pos = np.arange(SEQ, dtype=np.float64)
    ang = pos[None, :] * inv[:, None]
    cos = np.cos(ang)
    sin = np.sin(ang)
    CC = np.concatenate([cos, cos], axis=0).astype(np.float32)
    SS = np.concatenate([-sin, sin], axis=0).astype(np.float32)
    gam = 1.0 - 2.0 ** (-5.0 - np.arange(NH, dtype=np.float64))
    idx = np.arange(128)
    same = (idx[:, None] // 64) == (idx[None, :] // 64)
    dist = np.abs(idx[:, None] - idx[None, :]).astype(np.float64)
    sc = 128.0 ** -0.5
    DT = np.stack([np.where(same, gam[h] ** dist, 0.0) * sc for h in range(NH)], axis=1)
    QD = np.stack([np.broadcast_to((gam[h] ** (np.arange(64) + 1.0)) * sc, (128, 64)) for h in range(NH)], axis=1)
    KD = np.stack([gam[h] ** (63.0 - (idx % 64)) for h in range(NH)], axis=1)
    ident = np.eye(128)
    perm = np.zeros((128, 128))
    for dst in range(128):
        perm[(dst + 64) % 128, dst] = 1.0
    consts = np.concatenate([np.full((128, 128), 1.0 / D), np.full((128, 128), 1.0 / 128.0),
                             np.ones((128, 128)), ident, perm], axis=1)
    return CC, SS, DT.reshape(128, -1).astype(np.float32), QD.reshape(128, -1).astype(np.float32), \
        KD.astype(np.float32), consts.astype(np.float32)


def _bias_tiles(rel_bias):
    ki = np.arange(128)[:, None]
    qi = np.arange(128)[None, :]
    tiles = []
    for j in (0, 3, 4):
        distv = qi - ((j - 4) * 128 + ki)
        idx = np.clip(distv, -128, 128) + 128
        tiles.append(rel_bias[:, idx])
    bt = np.stack(tiles, axis=0)
    bt = np.transpose(bt, (2, 1, 0, 3))
    return np.ascontiguousarray(bt.reshape(128, -1)).astype(np.float32)


_CACHE = {}


def kernel(x, norm_ffn1_g, ffn1_w_gate, ffn1_w_up, ffn1_w_down, norm_mix_g, w_in, rel_bias,
           w_out_att, w_out_ret, w_out, norm_ffn2_g, ffn2_w_gate, ffn2_w_up, ffn2_w_down,
           norm_final_g, _n_pre=4, _n_own=4, _cores=8, _stop=None, _mode="cc"):
    x = np.asarray(x, dtype=np.float32)
    f = lambda a: np.ascontiguousarray(np.asarray(a, dtype=np.float32))
    key = (_n_pre, _n_own, _stop, _mode)
    if key not in _CACHE:
        _CACHE[key] = build_program(_n_pre, _n_own, _stop, _mode)
    nc = _CACHE[key]
    CC, SS, DT, QD, KD, consts = _host_tables()
    gains = np.stack([f(norm_ffn1_g)[0], f(norm_mix_g)[0], f(norm_ffn2_g)[0], f(norm_final_g)], axis=0)
    gains = np.ascontiguousarray(gains.reshape(4, 16, 128).transpose(2, 0, 1).reshape(128, 64))
    rb = f(rel_bias)[0]
    bias_t = _bias_tiles(rb)
    cb = np.ascontiguousarray(np.broadcast_to(rb[:, 256][None, :], (128, NH))).astype(np.float32)
    shared = {
        "ffn1_w_gate": f(ffn1_w_gate)[0], "ffn1_w_up": f(ffn1_w_up)[0], "ffn1_w_down": f(ffn1_w_down)[0],
        "w_in": f(w_in)[0], "w_out_att": f(w_out_att)[0], "w_out_ret": f(w_out_ret)[0], "w_out": f(w_out)[0],
        "ffn2_w_gate": f(ffn2_w_gate)[0], "ffn2_w_up": f(ffn2_w_up)[0], "ffn2_w_down": f(ffn2_w_down)[0],
        "gains": gains, "bias_t": bias_t, "cb": cb, "consts": consts, "dt_tab": DT, "qd_tab": QD, "kd_tab": KD,
    }
    in_maps = []
    for c in range(_cores):
        b, hf = c // 2, c % 2
        xo = np.ascontiguousarray(x[b, hf * HALF:(hf + 1) * HALF, :].T)
        m = dict(shared)
        m["x_own"] = xo
        m["prevmask"] = np.full((128, 1), 0.0 if hf == 1 else NEG, np.float32)
        m["flag"] = np.full((128, 1), 1.0 if hf == 1 else 0.0, np.float32)
        m["cc_own"] = np.ascontiguousarray(CC[:, hf * HALF:(hf + 1) * HALF])
        m["ss_own"] = np.ascontiguousarray(SS[:, hf * HALF:(hf + 1) * HALF])
        if _mode == "prefix":
            if hf == 1:
                xp = np.ascontiguousarray(x[b, 0:HALF, :].T)
            else:
                xp = np.zeros((D, HALF), np.float32)
            m["x_pre"] = xp
            m["cc_pre"] = np.ascontiguousarray(CC[:, 0:HALF])
            m["ss_pre"] = np.ascontiguousarray(SS[:, 0:HALF])
        in_maps.append(m)
    res = run_bass_kernel_spmd(nc, in_maps, core_ids=list(range(_cores)))
    out = np.empty((4, SEQ, D), np.float32)
    for c in range(_cores):
        b, hf = c // 2, c % 2
        out[b, hf * HALF:(hf + 1) * HALF, :] = res.results[c]["y"].T
    return out
```

```python
import numpy as np
from contextlib import ExitStack
import concourse.bass as bass
import concourse.mybir as mybir
from concourse.bass_utils import run_bass_kernel_spmd

F32 = mybir.dt.float32
BF16 = mybir.dt.bfloat16
AF = mybir.ActivationFunctionType
ALU = mybir.AluOpType

D = 2048
DC = 16
DFF = 5632
FC = 44
T = 512
SEQ = 4096
HALF = 2048
NH = 8
EPS = 1e-6
NEG = -30000.0
WSLOTS = 4
POOL_INFLIGHT = 3
SLOT_ELEMS = 4096
QA, KA, VA, QR, KR, VR, GR, GA, GRT = 0, 1024, 2048, 3072, 4096, 5120, 6144, 7168, 9216


class Res:
    __slots__ = ("w", "r", "excl")

    def __init__(self, excl=False):
        self.w = None
        self.r = {}
        self.excl = excl


class Builder:
    def __init__(self, nc, es):
        self.nc = nc
        self.es = es
        self.eng = {"pe": nc.tensor, "act": nc.scalar, "dve": nc.vector, "pool": nc.gpsimd, "sp": nc.sync}
        self.semobj = {}
        self.cnt = {}
        self.waited = {e: {} for e in self.eng}
        for e in self.eng:
            self.semobj[e] = es.enter_context(nc.semaphore("s_" + e))
            self.cnt[e] = 0
        self.dma_sems = {"sp": [], "pool": []}
        for q in ("sp", "pool"):
            for i in range(8):
                k = "d%s%d" % (q, i)
                self.semobj[k] = es.enter_context(nc.semaphore("s_" + k))
                self.cnt[k] = 0
                self.dma_sems[q].append(k)
        self.dma_rr = {"sp": 0, "pool": 0}
        for k in ("cc0", "cc1"):
            self.semobj[k] = es.enter_context(nc.semaphore("s_" + k))
            self.cnt[k] = 0
        self.last_tok = {}
        self.bank_rr = 0
        self.bank_live = set()
        self.out_toks = []
        self.pool_hist = []

    def _wait(self, e, tok):
        key, val = tok
        if self.waited[e].get(key, 0) >= val:
            return
        self.eng[e].wait_ge(self.semobj[key], val)
        self.waited[e][key] = val

    def _deps(self, e, reads, writes):
        for r in reads:
            if r.w is not None:
                t = r.w
                if t[0] == e and e == "pe":
                    continue
                self._wait(e, t)
            if r.excl:
                for k, t in r.r.items():
                    if t[0] != e:
                        self._wait(e, t)
        for w in writes:
            if w.w is not None and not (w.w[0] == e and e == "pe"):
                self._wait(e, w.w)
            for k, t in w.r.items():
                if not (t[0] == e and e == "pe"):
                    self._wait(e, t)

    def _commit(self, tok, rkey, reads, writes):
        for r in reads:
            r.r[rkey] = tok
        for w in writes:
            w.w = tok
            w.r = {}

    def op(self, e, fn, reads=(), writes=()):
        self._deps(e, reads, writes)
        inst = fn()
        self.cnt[e] += 1
        inst.then_inc(self.semobj[e], 1)
        tok = (e, self.cnt[e])
        self._commit(tok, e, reads, writes)
        return tok

    def pe_group(self, fns, reads=(), writes=()):
        self._deps("pe", reads, writes)
        inst = None
        for fn in fns:
            inst = fn()
        self.cnt["pe"] += 1
        inst.then_inc(self.semobj["pe"], 1)
        tok = ("pe", self.cnt["pe"])
        self._commit(tok, "pe", reads, writes)
        return tok

    def dma(self, q, out, in_, reads=(), writes=(), semkey=None):
        if semkey is None:
            semkey = self.dma_sems[q][self.dma_rr[q] % len(self.dma_sems[q])]
            self.dma_rr[q] += 1
        prev = self.last_tok.get(semkey)
        if prev is not None:
            self._wait(q, prev)
        if q == "pool":
            self.pool_hist.append(None)
            if len(self.pool_hist) > POOL_INFLIGHT and self.pool_hist[-1 - POOL_INFLIGHT] is not None:
                self._wait(q, self.pool_hist[-1 - POOL_INFLIGHT])
        self._deps(q, reads, writes)
        inst = self.eng[q].dma_start(out=out, in_=in_)
        self.cnt[semkey] += 16
        inst.then_inc(self.semobj[semkey], 16)
        tok = (semkey, self.cnt[semkey])
        self.last_tok[semkey] = tok
        if q == "pool":
            self.pool_hist[-1] = tok
        self._commit(tok, semkey, reads, writes)
        return tok

    def collective(self, semkey, fn, reads=(), writes=()):
        q = "pool"
        self._deps(q, reads, writes)
        inst = fn()
        self.cnt[semkey] += 1
        inst.then_inc(self.semobj[semkey], 1)
        tok = (semkey, self.cnt[semkey])
        self._commit(tok, semkey, reads, writes)
        return tok

    def bank(self):
        for _ in range(8):
            b = self.bank_rr % 8
            self.bank_rr += 1
            if b not in self.bank_live:
                self.bank_live.add(b)
                return b
        raise RuntimeError("out of PSUM banks")

    def free(self, *bs):
        for b in bs:
            self.bank_live.discard(b)


class Rot:
    def __init__(self, n, name):
        self.n = n
        self.rr = 0
        self.live = set()
        self.name = name

    def get(self):
        for _ in range(self.n):
            i = self.rr % self.n
            self.rr += 1
            if i not in self.live:
                self.live.add(i)
                return i
        raise RuntimeError("out of " + self.name)

    def free(self, *idx):
        for i in idx:
            self.live.discard(i)


def build_program(n_pre=4, n_own=4, stop=None, mode="cc"):
    nc = bass.Bass("TRN2", target_bir_lowering=False)

    def din(name, shape):
        return nc.dram_tensor(name, list(shape), F32, kind="ExternalInput").ap()

    x_own = din("x_own", [D, HALF])
    x_pre = din("x_pre", [D, HALF]) if mode == "prefix" else None
    W = {}
    for nm, shp in [("ffn1_w_gate", (D, DFF)), ("ffn1_w_up", (D, DFF)), ("ffn1_w_down", (DFF, D)),
                    ("w_in", (D, 11264)), ("w_out_att", (1024, D)), ("w_out_ret", (1024, D)),
                    ("w_out", (D, D)), ("ffn2_w_gate", (D, DFF)), ("ffn2_w_up", (D, DFF)),
                    ("ffn2_w_down", (DFF, D))]:
        W[nm] = din(nm, shp)
    gains_d = din("gains", [128, 64])
    bias_d = din("bias_t", [128, NH * 3 * 128])
    cb_d = din("cb", [128, NH])
    pm_d = din("prevmask", [128, 1])
    cc_own_d = din("cc_own", [128, HALF])
    ss_own_d = din("ss_own", [128, HALF])
    cc_pre_d = din("cc_pre", [128, HALF]) if mode == "prefix" else None
    ss_pre_d = din("ss_pre", [128, HALF]) if mode == "prefix" else None
    flag_d = din("flag", [128, 1])
    x1s = nc.dram_tensor("x1s", [D, HALF], F32)
    st_in = nc.dram_tensor("st_in", [128, NH * 128], F32)
    st_out = nc.dram_tensor("st_out", [256, NH * 128], F32)
    kv_in = nc.dram_tensor("kv_in", [128, 8192], BF16)
    kv_out = nc.dram_tensor("kv_out", [256, 8192], BF16)
    krs = nc.dram_tensor("krs", [4 * NH * 128, T], BF16)
    kds = nc.dram_tensor("kds", [4 * NH * 128, T], BF16)
    vrs = nc.dram_tensor("vrs", [4 * 128, 4096], BF16)
    consts_d = din("consts", [128, 5 * 128])
    dt_d = din("dt_tab", [128, NH * 128])
    qd_d = din("qd_tab", [128, NH * 64])
    kd_d = din("kd_tab", [128, NH])
    y = nc.dram_tensor("y", [D, HALF], F32, kind="ExternalOutput").ap()

    with ExitStack() as es:
        B = Builder(nc, es)

        def sb(name, n, dt):
            return es.enter_context(nc.sbuf_tensor(name, [128, n], dt))

        xres = sb("xres", DC * T, F32)
        xn = sb("xn", DC * T, BF16)
        hb = sb("hb", 32 * T, BF16)
        wring = sb("wring", WSLOTS * SLOT_ELEMS, BF16)
        kc = sb("kc", NH * 2 * T, BF16)
        vc = sb("vc", 2 * 4 * 1024, BF16)
        biasT = sb("biasT", NH * 3 * 128, F32)
        cbt = sb("cbt", NH, F32)
        cbm = sb("cbm", NH, F32)
        pm = sb("pm", 1, F32)
        epsb = sb("epsb", 1, F32)
        flag = sb("flag_s", 1, F32)
        gains = sb("gains_s", 64, F32)
        kdt = sb("kdt", NH, F32)
        dtt = sb("dtt", NH * 128, F32)
        qdt = sb("qdt", NH * 64, F32)
        consts = sb("consts_s", 5 * 128, BF16)
        state = sb("state", NH * 128, F32)
        sbf0 = sb("sbf0", NH * 128, BF16)
        NTF = 8
        tmpf = sb("tmpf", NTF * T, F32)
        NTB = 10
        tmpb = sb("tmpb", NTB * T, BF16)
        tabs = sb("tabs", 2 * T, F32)
        kdtok = sb("kdtok", 2 * 512, BF16)
        pT = sb("pT", 2 * 640, BF16)
        psum = es.enter_context(nc.psum_tensor("psum", [128, 8 * 512], F32))

        R_xres = [Res() for _ in range(DC)]
        R_xn = [Res() for _ in range(DC)]
        R_hb = [Res() for _ in range(32)]
        R_slot = [Res() for _ in range(WSLOTS)]
        R_kc = [[Res() for _ in range(2)] for _ in range(NH)]
        R_vc = [[Res() for _ in range(4)] for _ in range(2)]
        R_bank = [Res(excl=True) for _ in range(8)]
        R_const = Res()
        R_state = [Res() for _ in range(NH)]
        R_sbf0 = [Res() for _ in range(NH)]
        R_tmpf = [Res() for _ in range(NTF)]
        R_tmpb = [Res() for _ in range(NTB)]
        R_tabs = Res()
        R_kdtok = [Res() for _ in range(2)]
        R_pT = [Res() for _ in range(2)]
        rr = {"slot": 0, "kd": 0, "pt": 0}
        R_krs = [[Res() for _ in range(NH)] for _ in range(4)]
        R_kds = [[Res() for _ in range(NH)] for _ in range(4)]
        R_vrs = [Res() for _ in range(4)]

        def krs_ap(t, h):
            return krs.ap()[(t * NH + h) * 128:(t * NH + h + 1) * 128, :]

        def kds_ap(t, h):
            return kds.ap()[(t * NH + h) * 128:(t * NH + h + 1) * 128, :]

        def vrs_ap(t):
            return vrs.ap()[t * 128:(t + 1) * 128, :]
        TF = Rot(NTF, "tmpf")
        TB = Rot(NTB, "tmpb")

        def xres_c(c):
            return xres[:, c * T:(c + 1) * T]

        def xn_c(c):
            return xn[:, c * T:(c + 1) * T]

        def hb_c(c):
            return hb[:, c * T:(c + 1) * T]

        def ps(b, n=512, off=0):
            return psum[:, b * 512 + off: b * 512 + off + n]

        def new_tf():
            i = TF.get()
            return i, tmpf[:, i * T:(i + 1) * T], R_tmpf[i]

        def new_tb():
            i = TB.get()
            return i, tmpb[:, i * T:(i + 1) * T], R_tmpb[i]

        ONES_D = consts[:, 0:128]
        ONES_HD = consts[:, 128:256]
        ONES = consts[:, 256:384]
        IDENT = consts[:, 384:512]
        PERMT = consts[:, 512:640]

        for dst, src in [(gains, gains_d), (biasT, bias_d), (cbt, cb_d), (pm, pm_d), (kdt, kd_d),
                         (dtt, dt_d), (qdt, qd_d), (flag, flag_d)]:
            B.dma("sp", dst[:], src, writes=[R_const])
        B.dma("pool", consts[:], consts_d, writes=[R_const])
        bt3 = biasT[:].rearrange("p (h j q) -> p h j q", h=NH, j=3)
        B.op("dve", lambda: nc.vector.memset(bt3[0:64, :, 0, 64:128], NEG), reads=[R_const], writes=[R_const])
        B.op("dve", lambda: nc.vector.memset(bt3[64:128, :, 2, 0:64], NEG), reads=[R_const], writes=[R_const])
        B.op("dve", lambda: nc.vector.tensor_scalar(cbm[:], cbt[:], pm[:, 0:1], None, ALU.add),
             reads=[R_const], writes=[R_const])
        B.op("dve", lambda: nc.vector.memset(epsb[:], EPS), reads=[R_const], writes=[R_const])
        B.op("dve", lambda: nc.vector.memset(state[:], 0.0), writes=R_state)
        B.op("dve", lambda: nc.vector.memset(sbf0[:], 0.0), writes=R_sbf0)
        B.op("dve", lambda: nc.vector.memset(kc[:], 0.0), writes=[r for rs in R_kc for r in rs])
        B.op("dve", lambda: nc.vector.memset(vc[:], 0.0), writes=[r for rs in R_vc for r in rs])

        def wload(wap, row0, kcn, col0, ncols):
            s = rr["slot"] % WSLOTS
            rr["slot"] += 1
            view = wring[:, s * SLOT_ELEMS: s * SLOT_ELEMS + kcn * ncols].rearrange("p (k n) -> p k n", k=kcn)
            src = wap[row0:row0 + kcn * 128, col0:col0 + ncols].rearrange("(k p) n -> p k n", p=128)
            B.dma("pool", view, src, writes=[R_slot[s]])
            return view, R_slot[s]

        def rmsnorm(gidx, final=False):
            bnk = B.bank()
            for c in range(DC):
                if c % 2 == 0:
                    B.op("act", lambda c=c: nc.scalar.activation(hb_c(c), xres_c(c), AF.Square),
                         reads=[R_xres[c]], writes=[R_hb[c]])
                else:
                    B.op("dve", lambda c=c: nc.vector.tensor_tensor(hb_c(c), xres_c(c), xres_c(c), ALU.mult),
                         reads=[R_xres[c]], writes=[R_hb[c]])
            B.pe_group([lambda c=c: nc.tensor.matmul(ps(bnk), ONES_D, hb_c(c), start=(c == 0), stop=(c == DC - 1))
                        for c in range(DC)], reads=R_hb[0:DC] + [R_const], writes=[R_bank[bnk]])
            i_sd, sd, r_sd = new_tf()
            B.op("act", lambda: nc.scalar.activation(sd, ps(bnk), AF.Ln, bias=epsb[:, 0:1], scale=1.0),
                 reads=[R_bank[bnk], R_const], writes=[r_sd])
            B.free(bnk)
            i_rs, rs, r_rs = new_tf()
            B.op("act", lambda: nc.scalar.activation(rs, sd, AF.Exp, scale=-0.5), reads=[r_sd], writes=[r_rs])
            TF.free(i_sd)
            for c in range(DC):
                gcol = gains[:, gidx * 16 + c: gidx * 16 + c + 1]
                if final:
                    B.op("dve", lambda c=c, gcol=gcol: nc.vector.scalar_tensor_tensor(
                        xres_c(c), xres_c(c), gcol, rs, ALU.mult, ALU.mult),
                        reads=[R_xres[c], r_rs, R_const], writes=[R_xres[c]])
                else:
                    B.op("dve", lambda c=c, gcol=gcol: nc.vector.scalar_tensor_tensor(
                        xn_c(c), xres_c(c), gcol, rs, ALU.mult, ALU.mult),
                        reads=[R_xres[c], r_rs, R_const], writes=[R_xn[c]])
            TF.free(i_rs)

        def ffn(pfx, gidx):
            wg, wu, wd = W[pfx + "_w_gate"], W[pfx + "_w_up"], W[pfx + "_w_down"]
            rmsnorm(gidx)
            for (c0, nch) in ((0, 24), (24, 20)):
                for gi in range(nch // 4):
                    col0 = (c0 + gi * 4) * 128
                    gs = [wload(wg, r * 1024, 8, col0, 512) for r in range(2)]
                    us = [wload(wu, r * 1024, 8, col0, 512) for r in range(2)]
                    tfs = []
                    for mm in range(4):
                        gb = B.bank()
                        B.pe_group([lambda k=k: nc.tensor.matmul(
                            ps(gb), gs[k // 8][0][:, k % 8, mm * 128:(mm + 1) * 128], xn_c(k),
                            start=(k == 0), stop=(k == 15)) for k in range(16)],
                            reads=[gs[0][1], gs[1][1]] + R_xn, writes=[R_bank[gb]])
                        i_tf, tf, r_tf = new_tf()
                        B.op("act", lambda tf=tf, gb=gb: nc.scalar.activation(tf, ps(gb), AF.Silu),
                             reads=[R_bank[gb]], writes=[r_tf])
                        B.free(gb)
                        tfs.append((i_tf, tf, r_tf))
                    for mm in range(4):
                        ub = B.bank()
                        B.pe_group([lambda k=k: nc.tensor.matmul(
                            ps(ub), us[k // 8][0][:, k % 8, mm * 128:(mm + 1) * 128], xn_c(k),
                            start=(k == 0), stop=(k == 15)) for k in range(16)],
                            reads=[us[0][1], us[1][1]] + R_xn, writes=[R_bank[ub]])
                        i_tf, tf, r_tf = tfs[mm]
                        j = gi * 4 + mm
                        B.op("dve", lambda tf=tf, ub=ub, j=j: nc.vector.tensor_tensor(hb_c(j), tf, ps(ub), ALU.mult),
                             reads=[r_tf, R_bank[ub]], writes=[R_hb[j]])
                        B.free(ub)
                        TF.free(i_tf)
                nblk = (nch + 7) // 8
                for cg in range(4):
                    banks = [B.bank() for _ in range(4)]
                    for bi in range(nblk):
                        kcn = min(8, nch - bi * 8)
                        sv, sr = wload(wd, (c0 + bi * 8) * 128, kcn, cg * 512, 512)
                        fns = []
                        for mm in range(4):
                            for kk in range(kcn):
                                k = bi * 8 + kk
                                fns.append(lambda mm=mm, kk=kk, k=k, sv=sv: nc.tensor.matmul(
                                    ps(banks[mm]), sv[:, kk, mm * 128:(mm + 1) * 128], hb_c(k),
                                    start=(k == 0), stop=(k == nch - 1)))
                        B.pe_group(fns, reads=[sr] + R_hb[bi * 8: bi * 8 + kcn], writes=[R_bank[b] for b in banks])
                    for mm in range(4):
                        m = cg * 4 + mm
                        B.op("dve", lambda m=m, mm=mm: nc.vector.scalar_tensor_tensor(
                            xres_c(m), ps(banks[mm]), 0.5, xres_c(m), ALU.mult, ALU.add),
                            reads=[R_bank[banks[mm]], R_xres[m]], writes=[R_xres[m]])
                    B.free(*banks)

        def proj_block(col0, ncols, consume):
            if ncols == 512:
                ws = [wload(W["w_in"], r * 1024, 8, col0, 512) for r in range(2)]
                lhs = lambda k, mm: ws[k // 8][0][:, k % 8, mm * 128:(mm + 1) * 128]
                rds = [ws[0][1], ws[1][1]]
            else:
                wv, wr_ = wload(W["w_in"], 0, 16, col0, ncols)
                lhs = lambda k, mm: wv[:, k, mm * 128:(mm + 1) * 128]
                rds = [wr_]
            for mm in range(ncols // 128):
                b = B.bank()
                B.pe_group([lambda k=k: nc.tensor.matmul(ps(b), lhs(k, mm), xn_c(k),
                                                         start=(k == 0), stop=(k == 15)) for k in range(16)],
                           reads=rds + R_xn, writes=[R_bank[b]])
                consume(mm, b)
                B.free(b)

        def proj_tokmajor(col0, consume):
            wv, wr_ = wload(W["w_in"], 0, 16, col0, 256)
            for pair in range(2):
                b = B.bank()
                fns = []
                for tb in range(2):
                    tbi = pair * 2 + tb
                    for k in range(16):
                        fns.append(lambda k=k, tb=tb, tbi=tbi: nc.tensor.matmul(
                            ps(b, 256, tb * 256), xn[:, k * T + tbi * 128: k * T + (tbi + 1) * 128], wv[:, k, :],
                            start=(k == 0), stop=(k == 15)))
                B.pe_group(fns, reads=[wr_] + R_xn, writes=[R_bank[b]])
                consume(pair, b)
                B.free(b)

        def load_tabs(cc_d, ss_d, tok0):
            B.dma("sp", tabs[:, 0:T], cc_d[:, tok0:tok0 + T], writes=[R_tabs])
            B.dma("sp", tabs[:, T:2 * T], ss_d[:, tok0:tok0 + T], writes=[R_tabs])

        def rotary_from_psum(b, dst_ap, dst_res):
            i_xb, xb, r_xb = new_tb()
            B.op("act", lambda: nc.scalar.activation(xb, ps(b), AF.Copy), reads=[R_bank[b]], writes=[r_xb])
            b2 = B.bank()
            B.pe_group([lambda: nc.tensor.matmul(ps(b2), PERMT, xb, start=True, stop=True)],
                       reads=[r_xb, R_const], writes=[R_bank[b2]])
            i1, t1, r_t1 = new_tf()
            B.op("dve", lambda: nc.vector.tensor_tensor(t1, ps(b), tabs[:, 0:T], ALU.mult),
                 reads=[R_bank[b], R_tabs], writes=[r_t1])
            i2, t2, r_t2 = new_tf()
            B.op("dve", lambda: nc.vector.tensor_tensor(t2, ps(b2), tabs[:, T:2 * T], ALU.mult),
                 reads=[R_bank[b2], R_tabs], writes=[r_t2])
            B.op("dve", lambda: nc.vector.tensor_tensor(dst_ap, t1, t2, ALU.add),
                 reads=[r_t1, r_t2], writes=[dst_res])
            B.free(b2)
            TB.free(i_xb)
            TF.free(i1, i2)

        VR0 = 16
        SH0 = 24

        def vr_ap(blk, col0, n, p0=0, pn=128):
            base = VR0 * T + blk * 1024 + col0
            return hb[p0:p0 + pn, base: base + n]

        shf = hb[:, SH0 * T:(SH0 + 4) * T].bitcast(F32)
        shb_inst = [hb[:, (SH0 + 4) * T:(SH0 + 6) * T], hb[:, (SH0 + 6) * T:(SH0 + 8) * T]]
        R_vr = R_hb[VR0:VR0 + 8]
        R_shf = R_hb[SH0:SH0 + 4]
        R_shb_inst = [R_hb[SH0 + 4:SH0 + 6], R_hb[SH0 + 6:SH0 + 8]]

        def retention_state_update(h, krT, r_krT, want_hist, t=None, spill=False, preload=False):
            shb, R_shb = shb_inst[h % 2], R_shb_inst[h % 2]
            i = rr["kd"] % 2
            rr["kd"] += 1
            kdv = kdtok[:, i * 512:(i + 1) * 512]
            if preload:
                B.dma("sp", kdv, kds_ap(t, h), reads=[R_kds[t][h]], writes=[R_kdtok[i]])
            else:
                bt = B.bank()
                B.pe_group([lambda blk=blk: nc.tensor.matmul(ps(bt, 128, blk * 128), krT[:, blk * 128:(blk + 1) * 128],
                                                             IDENT, start=True, stop=True) for blk in range(4)],
                           reads=[r_krT, R_const], writes=[R_bank[bt]])
                B.op("act", lambda: nc.scalar.activation(kdv, ps(bt), AF.Copy, scale=kdt[:, h:h + 1]),
                     reads=[R_bank[bt], R_const], writes=[R_kdtok[i]])
                B.free(bt)
                if spill:
                    B.dma("sp", kds_ap(t, h), kdv, reads=[R_kdtok[i]], writes=[R_kds[t][h]])
            yield
            ub = [B.bank(), B.bank()]
            for g in range(2):
                fns = []
                hp = g * 64
                for cc in range(4):
                    blk = cc
                    fns.append(lambda cc=cc, blk=blk, hp=hp, g=g: nc.tensor.matmul(
                        ps(ub[g], 128, cc * 128),
                        kdtok[hp:hp + 64, i * 512 + blk * 128: i * 512 + (blk + 1) * 128],
                        vr_ap(blk, h * 128, 128, hp, 64), start=True, stop=True))
                B.pe_group(fns, reads=[R_kdtok[i]] + R_vr, writes=[R_bank[ub[g]]])
            yield
            cd = float((1.0 - 2.0 ** (-5.0 - h)) ** 64)
            st = state[:, h * 128:(h + 1) * 128]
            for c in range(8):
                g, cc = c % 2, c // 2
                dst = shf[:, c * 128:(c + 1) * 128]
                src = st if c == 0 else shf[:, (c - 1) * 128: c * 128]
                B.op("dve", lambda dst=dst, src=src, g=g, cc=cc: nc.vector.scalar_tensor_tensor(
                    dst, src, cd, ps(ub[g], 128, cc * 128), ALU.mult, ALU.add),
                    reads=[R_state[h], R_bank[ub[g]]] + R_shf, writes=R_shf)
            B.free(*ub)
            if want_hist:
                B.op("act", lambda: nc.scalar.activation(shb[:, 128:1024], shf[:, 0:896], AF.Copy),
                     reads=R_shf, writes=R_shb)
                B.op("act", lambda: nc.scalar.activation(shb[:, 0:128], sbf0[:, h * 128:(h + 1) * 128], AF.Copy),
                     reads=[R_sbf0[h]] + R_shb, writes=R_shb)
            B.op("act", lambda: nc.scalar.activation(st, shf[:, 896:1024], AF.Copy),
                 reads=R_shf, writes=[R_state[h]])
            B.op("act", lambda: nc.scalar.activation(sbf0[:, h * 128:(h + 1) * 128], shf[:, 896:1024], AF.Copy),
                 reads=R_shf, writes=[R_sbf0[h]])

        def load_vr():
            for cb2 in range(4):
                def consume(pair, b, cb2=cb2):
                    for tb in range(2):
                        blk = pair * 2 + tb
                        B.op("act", lambda tb=tb, blk=blk: nc.scalar.activation(
                            vr_ap(blk, cb2 * 256, 256), ps(b, 256, tb * 256), AF.Copy),
                            reads=[R_bank[b]], writes=R_vr)
                proj_tokmajor(VR + cb2 * 256, consume)

        def k_att_block4(cb4, hf):
            def consume(mm, b):
                h = cb4 * 4 + mm
                B.op("act", lambda: nc.scalar.activation(
                    kc[:, (h * 2 + hf) * T:(h * 2 + hf + 1) * T], ps(b), AF.Copy),
                    reads=[R_bank[b]], writes=[R_kc[h][hf]])
            proj_block(KA + cb4 * 512, 512, consume)

        def k_att_block(cb2, hf):
            if cb2 % 2 == 0:
                k_att_block4(cb2 // 2, hf)

        def v_att_block(cb2, hf):
            def consume(pair, b):
                for tb in range(2):
                    blk = pair * 2 + tb
                    B.op("act", lambda tb=tb, blk=blk: nc.scalar.activation(
                        vc[:, (hf * 4 + blk) * 1024 + cb2 * 256:(hf * 4 + blk) * 1024 + (cb2 + 1) * 256],
                        ps(b, 256, tb * 256), AF.Copy),
                        reads=[R_bank[b]], writes=[R_vc[hf][blk]])
            proj_tokmajor(VA + cb2 * 256, consume)

        def attention_head(h, hf, qT, r_qT, first_tile):
            scale = float(128 ** -0.5)
            ob = B.bank()
            db = B.bank()
            pidx = {0: 0, 3: 1, 4: 2, 1: 3, 2: 4}
            st_ = {}

            def scores(pr):
                bx, by = B.bank(), B.bank()
                pos = {0: (bx, 0), 3: (bx, 128), 4: (bx, 256), 1: (by, 0), 2: (by, 128)}
                blocks = {}
                fx, fy = [], []
                rd = [r_qT]
                for j in range(5):
                    bi = pr - 4 + j
                    if bi < 0:
                        hh, blk = 1 - hf, 4 + bi
                    else:
                        hh, blk = hf, bi
                    blocks[j] = (hh, blk)
                    kap = kc[:, (h * 2 + hh) * T + blk * 128:(h * 2 + hh) * T + (blk + 1) * 128]
                    bnk, off = pos[j]
                    f = (lambda kap=kap, bnk=bnk, off=off: nc.tensor.matmul(
                        ps(bnk, 128, off), kap, qT[:, pr * 128:(pr + 1) * 128], start=True, stop=True))
                    (fx if bnk == bx else fy).append(f)
                    if R_kc[h][hh] not in rd:
                        rd.append(R_kc[h][hh])
                B.pe_group(fx, reads=rd, writes=[R_bank[bx]])
                B.pe_group(fy, reads=rd, writes=[R_bank[by]])
                st_[pr] = (bx, by, blocks)

            def expo(pr):
                bx, by, blocks = st_[pr]
                pi = rr["pt"] % 2
                rr["pt"] += 1
                pbuf = pT[:, pi * 640:(pi + 1) * 640]
                i_tf, tf, r_tf = new_tf()
                B.op("dve", lambda: nc.vector.scalar_tensor_tensor(
                    tf[:, 0:384], ps(bx, 384), scale, biasT[:, h * 384:(h + 1) * 384], ALU.mult, ALU.add),
                    reads=[R_bank[bx], R_const], writes=[r_tf])
                if not first_tile:
                    B.op("act", lambda: nc.scalar.activation(pbuf[:, 0:384], tf[:, 0:384], AF.Exp),
                         reads=[r_tf], writes=[R_pT[pi]])
                    B.op("act", lambda: nc.scalar.activation(pbuf[:, 384:640], ps(by, 256), AF.Exp,
                                                             bias=cbt[:, h:h + 1], scale=scale),
                         reads=[R_bank[by], R_const], writes=[R_pT[pi]])
                else:
                    for j in (0, 3, 4):
                        c0 = pidx[j] * 128
                        if pr - 4 + j < 0:
                            B.op("act", lambda c0=c0: nc.scalar.activation(
                                pbuf[:, c0:c0 + 128], tf[:, c0:c0 + 128], AF.Exp, bias=pm[:, 0:1], scale=1.0),
                                reads=[r_tf, R_const], writes=[R_pT[pi]])
                        else:
                            B.op("act", lambda c0=c0: nc.scalar.activation(
                                pbuf[:, c0:c0 + 128], tf[:, c0:c0 + 128], AF.Exp),
                                reads=[r_tf], writes=[R_pT[pi]])
                    for j in (1, 2):
                        c0 = pidx[j] * 128
                        bcol = (cbm if (pr - 4 + j < 0) else cbt)[:, h:h + 1]
                        B.op("act", lambda c0=c0, bcol=bcol: nc.scalar.activation(
                            pbuf[:, c0:c0 + 128], ps(by, 128, c0 - 384), AF.Exp, bias=bcol, scale=scale),
                            reads=[R_bank[by], R_const], writes=[R_pT[pi]])
                B.free(bx, by)
                TF.free(i_tf)
                st_[pr] = (pi, pbuf, blocks)

            def pv(pr):
                pi, pbuf, blocks = st_[pr]
                fo, fd = [], []
                rdv = [R_pT[pi], R_const]
                for n_, j in enumerate((0, 3, 4, 1, 2)):
                    hh, blk = blocks[j]
                    vap = vc[:, (hh * 4 + blk) * 1024 + h * 128:(hh * 4 + blk) * 1024 + (h + 1) * 128]
                    pj = pbuf[:, pidx[j] * 128:(pidx[j] + 1) * 128]
                    fo.append(lambda vap=vap, pj=pj, n_=n_: nc.tensor.matmul(
                        ps(ob, 128, pr * 128), vap, pj, start=(n_ == 0), stop=(n_ == 4)))
                    fd.append(lambda pj=pj, n_=n_: nc.tensor.matmul(
                        ps(db, 128, pr * 128), ONES, pj, start=(n_ == 0), stop=(n_ == 4)))
                    if R_vc[hh][blk] not in rdv:
                        rdv.append(R_vc[hh][blk])
                B.pe_group(fo + fd, reads=rdv, writes=[R_bank[ob], R_bank[db]])

            scores(0)
            for pr in range(4):
                if pr + 1 < 4:
                    scores(pr + 1)
                expo(pr)
                pv(pr)
            i_rd, rd_, r_rd = new_tf()
            B.op("act", lambda: nc.scalar.activation(rd_, ps(db), AF.Ln), reads=[R_bank[db]], writes=[r_rd])
            B.op("act", lambda: nc.scalar.activation(rd_, rd_, AF.Exp, scale=-1.0), reads=[r_rd], writes=[r_rd])
            B.op("dve", lambda: nc.vector.tensor_tensor(hb_c(h), ps(ob), rd_, ALU.mult),
                 reads=[R_bank[ob], r_rd], writes=[R_hb[h]])
            B.free(ob, db)
            TF.free(i_rd)

        def retention_head(h, lite, t=None, spill=False, preload=False):
            shb, R_shb = shb_inst[h % 2], R_shb_inst[h % 2]
            res = {}

            def consume_k(mm, b):
                i_k, krT, r_krT = new_tb()
                rotary_from_psum(b, krT, r_krT)
                res["k"] = (i_k, krT, r_krT)

            if preload:
                i_k, krT, r_krT = new_tb()
                B.dma("sp", krT, krs_ap(t, h), reads=[R_krs[t][h]], writes=[r_krT])
            else:
                proj_block(KR + h * 128, 128, consume_k)
                i_k, krT, r_krT = res["k"]
                if spill:
                    B.dma("sp", krs_ap(t, h), krT, reads=[r_krT], writes=[R_krs[t][h]])
            yield
            if lite:
                yield from retention_state_update(h, krT, r_krT, want_hist=False, t=t, spill=spill)
                TB.free(i_k)
                return

            def consume_q(mm, b):
                i_q, qrT, r_qrT = new_tb()
                i_qd, qdT, r_qdT = new_tb()
                rotary_from_psum(b, qrT, r_qrT)
                qd3 = bass.AP(qdt, h * 64, [[NH * 64, 128], [0, 8], [1, 64]])
                B.op("dve", lambda: nc.vector.tensor_tensor(
                    qdT.rearrange("p (a b) -> p a b", a=8), qrT.rearrange("p (a b) -> p a b", a=8), qd3, ALU.mult),
                    reads=[r_qrT, R_const], writes=[r_qdT])
                res["q"] = (i_q, qrT, r_qrT, i_qd, qdT, r_qdT)

            proj_block(QR + h * 128, 128, consume_q)
            i_q, qrT, r_qrT, i_qd, qdT, r_qdT = res["q"]
            yield
            yield from retention_state_update(h, krT, r_krT, want_hist=True, t=t, preload=preload)
            yield
            sb_ = B.bank()
            B.pe_group([lambda pr=pr: nc.tensor.matmul(ps(sb_, 128, pr * 128), krT[:, pr * 128:(pr + 1) * 128],
                                                       qrT[:, pr * 128:(pr + 1) * 128], start=True, stop=True)
                        for pr in range(4)], reads=[r_krT, r_qrT], writes=[R_bank[sb_]])
            i_sm, sm, r_sm = new_tb()
            dt3 = bass.AP(dtt, h * 128, [[NH * 128, 128], [0, 4], [1, 128]])
            B.op("dve", lambda: nc.vector.tensor_tensor(
                sm.rearrange("p (a b) -> p a b", a=4), ps(sb_).rearrange("p (a b) -> p a b", a=4), dt3, ALU.mult),
                reads=[R_bank[sb_], R_const], writes=[r_sm])
            B.free(sb_)
            yield
            ob = B.bank()
            fns = []
            for pr in range(4):
                vap = vr_ap(pr, h * 128, 128)
                fns.append(lambda pr=pr, vap=vap: nc.tensor.matmul(
                    ps(ob, 128, pr * 128), vap, sm[:, pr * 128:(pr + 1) * 128], start=True, stop=False,
                    skip_group_check=True))
                for cc in range(2):
                    c = pr * 2 + cc
                    fns.append(lambda c=c, cc=cc: nc.tensor.matmul(
                        ps(ob, 64, c * 64), shb[:, c * 128:(c + 1) * 128], qdT[:, c * 64:(c + 1) * 64],
                        start=False, stop=True, skip_group_check=True))
            B.pe_group(fns, reads=R_vr + [r_sm, r_qdT] + R_shb, writes=[R_bank[ob]])
            TB.free(i_k, i_q, i_qd, i_sm)
            i_o, osb, r_osb = new_tf()
            B.op("act", lambda: nc.scalar.activation(osb, ps(ob), AF.Copy), reads=[R_bank[ob]], writes=[r_osb])
            i_sq, sq, r_sq = new_tb()
            B.op("act", lambda: nc.scalar.activation(sq, ps(ob), AF.Square), reads=[R_bank[ob]], writes=[r_sq])
            B.free(ob)
            yield
            nb = B.bank()
            B.pe_group([lambda: nc.tensor.matmul(ps(nb), ONES_HD, sq, start=True, stop=True)],
                       reads=[r_sq, R_const], writes=[R_bank[nb]])
            TB.free(i_sq)
            i_sd, sd, r_sd = new_tf()
            B.op("act", lambda: nc.scalar.activation(sd, ps(nb), AF.Ln, bias=epsb[:, 0:1], scale=1.0),
                 reads=[R_bank[nb], R_const], writes=[r_sd])
            B.free(nb)
            i_rs, rs, r_rs = new_tf()
            B.op("act", lambda: nc.scalar.activation(rs, sd, AF.Exp, scale=-0.5), reads=[r_sd], writes=[r_rs])
            TF.free(i_sd)
            i_on, on, r_on = new_tf()
            B.op("dve", lambda: nc.vector.tensor_tensor(on, osb, rs, ALU.mult), reads=[r_osb, r_rs], writes=[r_on])
            TF.free(i_o, i_rs)
            yield

            B.op("dve", lambda: nc.vector.scalar_tensor_tensor(hb_c(8 + h), on, 0.5, hb_c(8 + h), ALU.mult, ALU.mult),
                 reads=[r_on, R_hb[8 + h]], writes=[R_hb[8 + h]])
            TF.free(i_on)

        def gate_r_precompute():
            for cb4 in range(2):
                def consume_g(mm, b, cb4=cb4):
                    h = cb4 * 4 + mm
                    i_th, th, r_th = new_tf()
                    B.op("act", lambda: nc.scalar.activation(th, ps(b), AF.Tanh, scale=0.5),
                         reads=[R_bank[b]], writes=[r_th])
                    B.op("dve", lambda: nc.vector.scalar_tensor_tensor(hb_c(8 + h), th, 1.0, ps(b), ALU.add, ALU.mult),
                         reads=[r_th, R_bank[b]], writes=[R_hb[8 + h]])
                    TF.free(i_th)
                proj_block(GR + cb4 * 512, 512, consume_g)

        def run_interleaved(gens, width=2, stagger=4):
            active = []
            it = iter(gens)
            exhausted = False
            while True:
                if not exhausted and len(active) < width:
                    g = next(it, None)
                    if g is None:
                        exhausted = True
                    else:
                        if active:
                            for _ in range(stagger):
                                for a in list(active):
                                    try:
                                        next(a)
                                    except StopIteration:
                                        active.remove(a)
                        active.append(g)
                        continue
                if not active:
                    if exhausted:
                        break
                    continue
                for g in list(active):
                    try:
                        next(g)
                    except StopIteration:
                        active.remove(g)

        def merge_and_out():
            for g4 in range(4):
                c0 = g4 * 512
                gas = [wload(W["w_in"], r * 1024, 8, GA + c0, 512) for r in range(2)]
                av, ar_ = wload(W["w_out_att"], 0, 8, c0, 512)
                tas = []
                for mm in range(4):
                    cs = slice(mm * 128, (mm + 1) * 128)
                    ga = B.bank()
                    B.pe_group([lambda k=k: nc.tensor.matmul(ps(ga), gas[k // 8][0][:, k % 8, cs], xn_c(k),
                                                             start=(k == 0), stop=(k == 15)) for k in range(16)],
                               reads=[gas[0][1], gas[1][1]] + R_xn, writes=[R_bank[ga]])
                    i_ta, ta, r_ta = new_tf()
                    B.op("act", lambda ta=ta, ga=ga: nc.scalar.activation(ta, ps(ga), AF.Tanh, scale=0.5),
                         reads=[R_bank[ga]], writes=[r_ta])
                    B.free(ga)
                    tas.append((i_ta, ta, r_ta))
                for mm in range(4):
                    cs = slice(mm * 128, (mm + 1) * 128)
                    ba = B.bank()
                    B.pe_group([lambda k=k: nc.tensor.matmul(ps(ba), av[:, k, cs], hb_c(k), start=(k == 0), stop=(k == 7))
                                for k in range(8)], reads=[ar_] + R_hb[0:8], writes=[R_bank[ba]])
                    i_ta, ta, r_ta = tas[mm]
                    B.op("dve", lambda ta=ta, ba=ba: nc.vector.scalar_tensor_tensor(ta, ta, 1.0, ps(ba), ALU.add, ALU.mult),
                         reads=[r_ta, R_bank[ba]], writes=[r_ta])
                    B.free(ba)
                grs = [wload(W["w_in"], r * 1024, 8, GRT + c0, 512) for r in range(2)]
                rv, rr_ = wload(W["w_out_ret"], 0, 8, c0, 512)
                trs = []
                for mm in range(4):
                    cs = slice(mm * 128, (mm + 1) * 128)
                    gr = B.bank()
                    B.pe_group([lambda k=k: nc.tensor.matmul(ps(gr), grs[k // 8][0][:, k % 8, cs], xn_c(k),
                                                             start=(k == 0), stop=(k == 15)) for k in range(16)],
                               reads=[grs[0][1], grs[1][1]] + R_xn, writes=[R_bank[gr]])
                    i_tr, tr, r_tr = new_tf()
                    B.op("act", lambda tr=tr, gr=gr: nc.scalar.activation(tr, ps(gr), AF.Tanh, scale=0.5),
                         reads=[R_bank[gr]], writes=[r_tr])
                    B.free(gr)
                    trs.append((i_tr, tr, r_tr))
                for mm in range(4):
                    cs = slice(mm * 128, (mm + 1) * 128)
                    m = g4 * 4 + mm
                    br = B.bank()
                    B.pe_group([lambda k=k: nc.tensor.matmul(ps(br), rv[:, k, cs], hb_c(8 + k), start=(k == 0), stop=(k == 7))
                                for k in range(8)], reads=[rr_] + R_hb[8:16], writes=[R_bank[br]])
                    i_tr, tr, r_tr = trs[mm]
                    i_ta, ta, r_ta = tas[mm]
                    B.op("dve", lambda tr=tr, br=br: nc.vector.scalar_tensor_tensor(tr, tr, 1.0, ps(br), ALU.add, ALU.mult),
                         reads=[r_tr, R_bank[br]], writes=[r_tr])
                    B.free(br)
                    B.op("dve", lambda m=m, ta=ta, tr=tr: nc.vector.tensor_tensor(hb_c(16 + m), ta, tr, ALU.add),
                         reads=[r_ta, r_tr], writes=[R_hb[16 + m]])
                    TF.free(i_ta, i_tr)
            for g4 in range(4):
                ovs = [wload(W["w_out"], r * 1024, 8, g4 * 512, 512) for r in range(2)]
                for mm in range(4):
                    m = g4 * 4 + mm
                    cs = slice(mm * 128, (mm + 1) * 128)
                    b = B.bank()
                    B.pe_group([lambda k=k: nc.tensor.matmul(ps(b), ovs[k // 8][0][:, k % 8, cs], hb_c(16 + k),
                                                             start=(k == 0), stop=(k == 15)) for k in range(16)],
                               reads=[ovs[0][1], ovs[1][1]] + R_hb[16:32], writes=[R_bank[b]])
                    B.op("dve", lambda m=m, b=b: nc.vector.scalar_tensor_tensor(
                        xres_c(m), ps(b), 0.5, xres_c(m), ALU.mult, ALU.add),
                        reads=[R_bank[b], R_xres[m]], writes=[R_xres[m]])
                    B.free(b)

        def xfer_x(dram_ap, tok0, to_sbuf, extra_reads=(), extra_writes=()):
            toks = []
            for g in range(4):
                sb_ap = xres[:, g * 4 * T:(g + 1) * 4 * T].rearrange("p (c t) -> p c t", c=4)
                dr_ap = dram_ap[g * 512:(g + 1) * 512, tok0:tok0 + T].rearrange("(c p) t -> p c t", p=128)
                if to_sbuf:
                    toks.append(B.dma("sp", sb_ap, dr_ap, reads=list(extra_reads), writes=R_xres[g * 4:(g + 1) * 4]))
                else:
                    toks.append(B.dma("sp", dr_ap, sb_ap, reads=R_xres[g * 4:(g + 1) * 4], writes=list(extra_writes)))
            return toks

        def load_x(src, tok0):
            xfer_x(src, tok0, True)

        def x_tile_ap(dram_ap, t):
            return dram_ap[:, t * T:(t + 1) * T].rearrange("(c p) t -> p c t", p=128)

        xres3 = xres[:].rearrange("p (c t) -> p c t", c=DC)

        def own_tile_phase2(t, after_kv=None):
            hf = t % 2
            for cb2 in range(4):
                k_att_block(cb2, hf)
                v_att_block(cb2, hf)
            if after_kv is not None:
                after_kv()
            for cb4 in range(2):
                def consume_q(mm, b, cb4=cb4, t=t, hf=hf):
                    h = cb4 * 4 + mm
                    i_q, qT, r_qT = new_tb()
                    B.op("act", lambda: nc.scalar.activation(qT, ps(b), AF.Copy), reads=[R_bank[b]], writes=[r_qT])
                    attention_head(h, hf, qT, r_qT, first_tile=(t == 0))
                    TB.free(i_q)
                proj_block(QA + cb4 * 512, 512, consume_q)
            gate_r_precompute()
            if mode == "cc":
                B.dma("sp", hb[:, VR0 * T:VR0 * T + 4096], vrs_ap(t), reads=[R_vrs[t]], writes=R_vr)
                run_interleaved([retention_head(h, lite=False, t=t, preload=True) for h in range(NH)])
            else:
                load_vr()
                run_interleaved([retention_head(h, lite=False) for h in range(NH)])
            merge_and_out()
            ffn("ffn2", 2)
            rmsnorm(3, final=True)
            B.out_toks.extend(xfer_x(y, t * T, False))

        if mode == "cc":
            R_x1s = [Res() for _ in range(4)]
            R_stin, R_stout, R_kvin, R_kvout = Res(), Res(), Res(), Res()
            load_x(x_own, 0)
            for t in range(n_own):
                hf = t % 2
                load_tabs(cc_own_d, ss_own_d, t * T)
                ffn("ffn1", 0)
                xfer_x(x1s.ap(), t * T, False, extra_writes=[R_x1s[t]])
                rmsnorm(1)
                if t + 1 < n_own:
                    load_x(x_own, (t + 1) * T)
                if t == n_own - 1:
                    for cb2 in range(4):
                        k_att_block(cb2, 1)
                        v_att_block(cb2, 1)
                load_vr()
                B.dma("sp", vrs_ap(t), hb[:, VR0 * T:VR0 * T + 4096], reads=R_vr, writes=[R_vrs[t]])
                run_interleaved([retention_head(h, lite=True, t=t, spill=True) for h in range(NH)])
            kc4 = kc[:].rearrange("p (h f t) -> p h f t", h=NH, f=2)
            B.dma("sp", st_in.ap(), state[:], reads=R_state, writes=[R_stin])
            B.dma("sp", kv_in.ap()[:, 0:4096].rearrange("p (h t) -> p h t", h=NH), kc4[:, :, 1, :],
                  reads=[R_kc[h][1] for h in range(NH)], writes=[R_kvin])
            B.dma("sp", kv_in.ap()[:, 4096:8192], vc[:, 4096:8192], reads=R_vc[1], writes=[R_kvin])
            groups = [[0, 1], [2, 3], [4, 5], [6, 7]]
            B.collective("cc0", lambda: nc.gpsimd.collective_compute(
                "AllGather", ALU.bypass, replica_groups=groups, ins=[st_in.ap().opt()], outs=[st_out.ap().opt()]),
                reads=[R_stin], writes=[R_stout])
            B.collective("cc1", lambda: nc.gpsimd.collective_compute(
                "AllGather", ALU.bypass, replica_groups=groups, ins=[kv_in.ap().opt()], outs=[kv_out.ap().opt()]),
                reads=[R_kvin], writes=[R_kvout])
            def do_import():
                B.dma("sp", state[:], st_out.ap()[0:128, :], reads=[R_stout], writes=R_state)
                B.dma("sp", kc4[:, :, 1, :], kv_out.ap()[0:128, 0:4096].rearrange("p (h t) -> p h t", h=NH),
                      reads=[R_kvout], writes=[R_kc[h][1] for h in range(NH)])
                B.dma("sp", vc[:, 4096:8192], kv_out.ap()[0:128, 4096:8192], reads=[R_kvout], writes=R_vc[1])
                B.op("dve", lambda: nc.vector.tensor_scalar(state[:], state[:], flag[:, 0:1], None, ALU.mult),
                     reads=R_state + [R_const], writes=R_state)
                B.op("act", lambda: nc.scalar.activation(sbf0[:], state[:], AF.Copy), reads=R_state, writes=R_sbf0)
            for t in range(n_own):
                xfer_x(x1s.ap(), t * T, True, extra_reads=[R_x1s[t]])
                load_tabs(cc_own_d, ss_own_d, t * T)
                rmsnorm(1)
                own_tile_phase2(t, after_kv=(do_import if t == 0 else None))
            for tok in B.out_toks:
                B._wait("sp", tok)
            return nc

        for t in range(4 - n_pre, 4):
            hf = t % 2
            load_x(x_pre, t * T)
            load_tabs(cc_pre_d, ss_pre_d, t * T)
            ffn("ffn1", 0)
            rmsnorm(1)
            if t == 3:
                for cb2 in range(4):
                    k_att_block(cb2, hf)
                    v_att_block(cb2, hf)
            load_vr()
            run_interleaved([retention_head(h, lite=True) for h in range(NH)])

        class _Stop(Exception):
            pass

        def finish(t):
            tok = B.dma("sp", y[:, t * T:(t + 1) * T].rearrange("(c p) t -> p c t", p=128),
                        xres[:].rearrange("p (c t) -> p c t", c=DC), reads=R_xres)
            B.out_toks.append(tok)
            for tok in B.out_toks:
                B._wait("sp", tok)

        for t in range(n_own):
            hf = t % 2
            load_x(x_own, t * T)
            if stop == "load":
                finish(t)
                return nc
            load_tabs(cc_own_d, ss_own_d, t * T)
            if stop == "norm0":
                rmsnorm(0, final=True)
                finish(t)
                return nc
            ffn("ffn1", 0)
            if stop == "ffn1":
                finish(t)
                return nc
            rmsnorm(1)
            if stop == "vr2":
                load_vr()
                finish(t)
                return nc
            if stop == "vr4":
                for cb2 in range(4):
                    k_att_block(cb2, hf)
                    v_att_block(cb2, hf)
                for cb2 in range(4):
                    k_att_block(cb2, hf)
                    v_att_block(cb2, hf)
                finish(t)
                return nc
            if stop == "vr6":
                ffn("ffn1", 0)
                finish(t)
                return nc
            if stop == "vr3":
                for cb2 in range(4):
                    k_att_block(cb2, hf)
                    v_att_block(cb2, hf)
                for cb2 in range(4):
                    k_att_block(cb2, hf)
                    v_att_block(cb2, hf)
                for cb2 in range(4):
                    k_att_block(cb2, hf)
                    v_att_block(cb2, hf)
                finish(t)
                return nc
            for cb2 in range(4):
                k_att_block(cb2, hf)
                v_att_block(cb2, hf)
            for cb2 in range(4):
                def consume_q(mm, b, cb2=cb2, t=t, hf=hf):
                    h = cb2 * 2 + mm
                    i_q, qT, r_qT = new_tb()
                    B.op("act", lambda: nc.scalar.activation(qT, ps(b), AF.Copy), reads=[R_bank[b]], writes=[r_qT])
                    attention_head(h, hf, qT, r_qT, first_tile=(t == 0))
                    TB.free(i_q)
                proj_block(QA + cb2 * 256, 256, consume_q)
            if stop == "att":
                finish(t)
                return nc
            gate_r_precompute()
            load_vr()
            if stop == "vr":
                finish(t)
                return nc
            if stop in ("rl", "rl1", "rl2", "rl3"):
                run_interleaved([retention_head(0, lite=True)])
                finish(t)
                return nc
            if stop == "r0":
                run_interleaved([retention_head(0, lite=False)])
                finish(t)
                return nc
            run_interleaved([retention_head(h, lite=False) for h in range(NH)])
            if stop == "ret":
                finish(t)
                return nc
            merge_and_out()
            ffn("ffn2", 2)
            rmsnorm(3, final=True)
            tok = B.dma("sp", y[:, t * T:(t + 1) * T].rearrange("(c p) t -> p c t", p=128),
                        xres[:].rearrange("p (c t) -> p c t", c=DC), reads=R_xres)
            B.out_toks.append(tok)
        for tok in B.out_toks:
            B._wait("sp", tok)
    return nc


def _host_tables():
    inv = 1.0 / (10000.0 ** (np.arange(0, 128, 2, dtype=np.float64) / 128.0))
    pos = np.arange(SEQ, dtype=np.float64)
    ang = pos[None, :] * inv[:, None]
    cos = np.cos(ang)
    sin = np.sin(ang)
    CC = np.concatenate([cos, cos], axis=0).astype(np.float32)
    SS = np.concatenate([-sin, sin], axis=0).astype(np.float32)
    gam = 1.0 - 2.0 ** (-5.0 - np.arange(NH, dtype=np.float64))
    idx = np.arange(128)
    same = (idx[:, None] // 64) == (idx[None, :] // 64)
    dist = np.abs(idx[:, None] - idx[None, :]).astype(np.float64)
    sc = 128.0 ** -0.5
    DT = np.stack([np.where(same, gam[h] ** dist, 0.0) * sc for h in range(NH)], axis=1)
    QD = np.stack([np.broadcast_to((gam[h] ** (np.arange(64) + 1.0)) * sc, (128, 64)) for h in range(NH)], axis=1)
    KD = np.stack([gam[h] ** (63.0 - (idx % 64)) for h in range(NH)], axis=1)
    ident = np.eye(128)
    perm = np.zeros((128, 128))
    for dst in range(128):
        perm[(dst + 64) % 128, dst] = 1.0
    consts = np.concatenate([np.full((128, 128), 1.0 / D), np.full((128, 128), 1.0 / 128.0),
                             np.ones((128, 128)), ident, perm], axis=1)
    return CC, SS, DT.reshape(128, -1).astype(np.float32), QD.reshape(128, -1).astype(np.float32), \
        KD.astype(np.float32), consts.astype(np.float32)


def _bias_tiles(rel_bias):
    ki = np.arange(128)[:, None]
    qi = np.arange(128)[None, :]
    tiles = []
    for j in (0, 3, 4):
        distv = qi - ((j - 4) * 128 + ki)
        idx = np.clip(distv, -128, 128) + 128
        tiles.append(rel_bias[:, idx])
    bt = np.stack(tiles, axis=0)
    bt = np.transpose(bt, (2, 1, 0, 3))
    return np.ascontiguousarray(bt.reshape(128, -1)).astype(np.float32)


_CACHE = {}


def kernel(x, norm_ffn1_g, ffn1_w_gate, ffn1_w_up, ffn1_w_down, norm_mix_g, w_in, rel_bias,
           w_out_att, w_out_ret, w_out, norm_ffn2_g, ffn2_w_gate, ffn2_w_up, ffn2_w_down,
           norm_final_g, _n_pre=4, _n_own=4, _cores=8, _stop=None, _mode="cc"):
    x = np.asarray(x, dtype=np.float32)
    f = lambda a: np.ascontiguousarray(np.asarray(a, dtype=np.float32))
    key = (_n_pre, _n_own, _stop, _mode)
    if key not in _CACHE:
        _CACHE[key] = build_program(_n_pre, _n_own, _stop, _mode)
    nc = _CACHE[key]
    CC, SS, DT, QD, KD, consts = _host_tables()
    gains = np.stack([f(norm_ffn1_g)[0], f(norm_mix_g)[0], f(norm_ffn2_g)[0], f(norm_final_g)], axis=0)
    gains = np.ascontiguousarray(gains.reshape(4, 16, 128).transpose(2, 0, 1).reshape(128, 64))
    rb = f(rel_bias)[0]
    bias_t = _bias_tiles(rb)
    cb = np.ascontiguousarray(np.broadcast_to(rb[:, 256][None, :], (128, NH))).astype(np.float32)
    shared = {
        "ffn1_w_gate": f(ffn1_w_gate)[0], "ffn1_w_up": f(ffn1_w_up)[0], "ffn1_w_down": f(ffn1_w_down)[0],
        "w_in": f(w_in)[0], "w_out_att": f(w_out_att)[0], "w_out_ret": f(w_out_ret)[0], "w_out": f(w_out)[0],
        "ffn2_w_gate": f(ffn2_w_gate)[0], "ffn2_w_up": f(ffn2_w_up)[0], "ffn2_w_down": f(ffn2_w_down)[0],
        "gains": gains, "bias_t": bias_t, "cb": cb, "consts": consts, "dt_tab": DT, "qd_tab": QD, "kd_tab": KD,
    }
    in_maps = []
    for c in range(_cores):
        b, hf = c // 2, c % 2
        xo = np.ascontiguousarray(x[b, hf * HALF:(hf + 1) * HALF, :].T)
        m = dict(shared)
        m["x_own"] = xo
        m["prevmask"] = np.full((128, 1), 0.0 if hf == 1 else NEG, np.float32)
        m["flag"] = np.full((128, 1), 1.0 if hf == 1 else 0.0, np.float32)
        m["cc_own"] = np.ascontiguousarray(CC[:, hf * HALF:(hf + 1) * HALF])
        m["ss_own"] = np.ascontiguousarray(SS[:, hf * HALF:(hf + 1) * HALF])
        if _mode == "prefix":
            if hf == 1:
                xp = np.ascontiguousarray(x[b, 0:HALF, :].T)
            else:
                xp = np.zeros((D, HALF), np.float32)
            m["x_pre"] = xp
            m["cc_pre"] = np.ascontiguousarray(CC[:, 0:HALF])
            m["ss_pre"] = np.ascontiguousarray(SS[:, 0:HALF])
        in_maps.append(m)
    res = run_bass_kernel_spmd(nc, in_maps, core_ids=list(range(_cores)))
    out = np.empty((4, SEQ, D), np.float32)
    for c in range(_cores):
        b, hf = c // 2, c % 2
        out[b, hf * HALF:(hf + 1) * HALF, :] = res.results[c]["y"].T
    return out
```
